# Optimizing a Trainium2 kernel written in Bass

```python
import functools
import jax, jax.numpy as jnp
from jax import lax
import numpy as np

D_MODEL = 1024
BATCH = 4
SEQ = 8192
DEPTH = 1
DEC_BATCH = 8
DEC_SEQ = 32
PAST_LEN = 4096

CHUNK = 64
D_MIX = D_MODEL
M_HEADS = 4
M_DIM = (D_MIX // 2) // M_HEADS
A_HEADS = 8
A_KV_HEADS = 2
A_GROUP = A_HEADS // A_KV_HEADS
A_DIM = (D_MIX // 2) // A_HEADS
WINDOW = 128
WIN_CHUNKS = WINDOW // CHUNK
D_FF = 4 * D_MODEL
EPS = 1e-6
M_W = M_HEADS * M_DIM
A_QW = A_HEADS * A_DIM
A_KW = A_KV_HEADS * A_DIM
SPLITS = (M_W, M_W, M_W, M_W, M_HEADS, M_HEADS, A_QW, A_KW, A_KW)
N_IN = sum(SPLITS)

kernel_name = 'hybrid_mlstm_swa_stream_step'


def rmsnorm(x, g):
    xf = x.astype(jnp.float32)
    y = xf * lax.rsqrt(jnp.mean(xf * xf, axis=-1, keepdims=True) + EPS)
    return (y * g.astype(jnp.float32)).astype(x.dtype)


def alibi_slopes():
    return jnp.exp2(-8.0 * jnp.arange(1, A_HEADS + 1, dtype=jnp.float32) / A_HEADS)


def ada_modulation(c, w_ada, b_ada, dtype):
    mod = jax.nn.silu(c.astype(jnp.float32)) @ w_ada.astype(jnp.float32) + b_ada.astype(jnp.float32)
    return [m[:, None, :].astype(dtype) for m in jnp.split(mod, 6, axis=-1)]


def mlstm_chunkwise(q, k, v, log_i, log_f, c0, n0, m0, chunk):
    B, L, H, _ = q.shape
    DV = v.shape[-1]
    nc = L // chunk

    def blocks(a):
        return a.reshape(B, nc, chunk, H, a.shape[-1]).transpose(1, 0, 3, 2, 4)

    def gblocks(a):
        return a.reshape(B, nc, chunk, H).transpose(1, 0, 3, 2)

    causal = jnp.tril(jnp.ones((chunk, chunk), dtype=bool))

    def step(carry, inp):
        c, n, m = carry
        qc, kc, vc, li, lf = inp
        b = jnp.cumsum(lf, axis=-1)
        dmat = jnp.where(causal, b[..., :, None] - b[..., None, :] + li[..., None, :], -jnp.inf)
        inter = b + m[..., None]
        m_t = jnp.maximum(inter, jnp.max(dmat, axis=-1))
        w_intra = jnp.exp(dmat - m_t[..., None])
        w_inter = jnp.exp(inter - m_t)
        s = jnp.einsum('bhtd,bhsd->bhts', qc, kc) * w_intra
        num = jnp.einsum('bhts,bhsv->bhtv', s, vc) + w_inter[..., None] * jnp.einsum('bhtd,bhdv->bhtv', qc, c)
        den = jnp.sum(s, axis=-1) + w_inter * jnp.einsum('bhtd,bhd->bht', qc, n)
        h = num / jnp.maximum(jnp.abs(den), jnp.exp(-m_t))[..., None]
        m_new = m_t[..., -1]
        w_src = jnp.exp(b[..., -1:] - b + li - m_new[..., None])
        w_old = jnp.exp(b[..., -1] + m - m_new)
        c_new = w_old[..., None, None] * c + jnp.einsum('bhs,bhsd,bhsv->bhdv', w_src, kc, vc)
        n_new = w_old[..., None] * n + jnp.einsum('bhs,bhsd->bhd', w_src, kc)
        return (c_new, n_new, m_new), h

    (c, n, m), h = lax.scan(step, (c0, n0, m0),
                            (blocks(q), blocks(k), blocks(v), gblocks(log_i), gblocks(log_f)))
    h = h.transpose(1, 0, 3, 2, 4).reshape(B, L, H, DV)
    return h, c, n, m


def band_attention(qb, kb, vb, q_pos, k_pos, sinks):
    slopes = alibi_slopes().reshape(A_KV_HEADS, A_GROUP)
    qc = q_pos // CHUNK
    kc = k_pos // CHUNK
    allowed = ((kc[:, None, :] <= qc[:, :, None]) & (kc[:, None, :] >= qc[:, :, None] - WIN_CHUNKS)
               & (k_pos[:, None, :] >= 0))
    dist = jnp.abs(q_pos[:, :, None] - k_pos[:, None, :]).astype(jnp.float32)
    logits = jnp.einsum('bntkgd,bnskd->bnkgts', qb.astype(jnp.float32), kb.astype(jnp.float32)) * (A_DIM ** -0.5)
    logits = logits - slopes[None, None, :, :, None, None] * dist[None, :, None, None, :, :]
    logits = jnp.where(allowed[None, :, None, None, :, :], logits, -jnp.inf)
    sink = sinks.astype(jnp.float32).reshape(A_KV_HEADS, A_GROUP)[None, None, :, :, None, None]
    mx = jnp.maximum(jnp.max(logits, axis=-1, keepdims=True), sink)
    p = jnp.exp(logits - mx)
    probs = p / (jnp.sum(p, axis=-1, keepdims=True) + jnp.exp(sink - mx))
    return jnp.einsum('bnkgts,bnskd->bntkgd', probs, vb.astype(jnp.float32))


def swa_prompt(aq, ak, av, sinks):
    B, L = aq.shape[:2]
    nb = L // CHUNK
    pad = WIN_CHUNKS * CHUNK
    qb = aq.reshape(B, nb, CHUNK, A_KV_HEADS, A_GROUP, A_DIM)

    def band(a):
        ap = jnp.pad(a, ((0, 0), (pad, 0), (0, 0), (0, 0))).reshape(B, nb + WIN_CHUNKS, CHUNK, A_KV_HEADS, A_DIM)
        return jnp.concatenate([ap[:, j:j + nb] for j in range(WIN_CHUNKS + 1)], axis=2)

    q_pos = jnp.arange(L, dtype=jnp.int32).reshape(nb, CHUNK)
    k_pos = ((jnp.arange(nb, dtype=jnp.int32)[:, None] - WIN_CHUNKS) * CHUNK
             + jnp.arange((WIN_CHUNKS + 1) * CHUNK, dtype=jnp.int32)[None, :])
    out = band_attention(qb, band(ak), band(av), q_pos, k_pos, sinks)
    return out.reshape(B, L, A_QW)


def prompt_mixers(mq, mk, mv, log_i, log_f, aq, ak, av, sinks):
    B = mq.shape[0]
    c0 = jnp.zeros((B, M_HEADS, M_DIM, M_DIM), jnp.float32)
    n0 = jnp.zeros((B, M_HEADS, M_DIM), jnp.float32)
    m0 = jnp.zeros((B, M_HEADS), jnp.float32)
    h_m, c, n, m = mlstm_chunkwise(mq, mk, mv, log_i, log_f, c0, n0, m0, CHUNK)
    attn = swa_prompt(aq, ak, av, sinks)
    return h_m, attn, (ak[:, -WINDOW:], av[:, -WINDOW:], c, n, m)


def sample_mixers(mq, mk, mv, log_i, log_f, aq, ak, av, sinks, cache_k, cache_v, st_c, st_n, st_m):
    B, T = mq.shape[:2]
    h_m, c, n, m = mlstm_chunkwise(mq, mk, mv, log_i, log_f, st_c.astype(jnp.float32),
                                   st_n.astype(jnp.float32), st_m.astype(jnp.float32), T)
    wb = cache_k.shape[1]
    k_all = jnp.concatenate([cache_k.astype(ak.dtype), ak], axis=1)
    v_all = jnp.concatenate([cache_v.astype(av.dtype), av], axis=1)
    q_pos = (PAST_LEN + jnp.arange(T, dtype=jnp.int32))[None, :]
    k_pos = (PAST_LEN - wb + jnp.arange(wb + T, dtype=jnp.int32))[None, :]
    attn = band_attention(aq.reshape(B, 1, T, A_KV_HEADS, A_GROUP, A_DIM), k_all[:, None], v_all[:, None],
                          q_pos, k_pos, sinks).reshape(B, T, A_QW)
    return h_m, attn, (k_all[:, -wb:], v_all[:, -wb:], c, n, m)


def trunk_layer(x, c, mixers, w_ada, b_ada, g_norm1, w_in, b_gates, g_q, g_k, sinks,
                g_mlstm_out, w_out, g_norm2, w_up, w_down):
    B, L, _ = x.shape
    sh1, sc1, ga1, sh2, sc2, ga2 = ada_modulation(c, w_ada, b_ada, x.dtype)
    h = rmsnorm(x, g_norm1) * (1 + sc1) + sh1
    z = h @ w_in
    offsets = [int(o) for o in np.cumsum(SPLITS)[:-1]]
    mq, mk, mv, mo, mi, mf, aq, ak, av = jnp.split(z, offsets, axis=-1)
    f32 = jnp.float32
    mq = mq.astype(f32).reshape(B, L, M_HEADS, M_DIM)
    mk = mk.astype(f32).reshape(B, L, M_HEADS, M_DIM) * (M_DIM ** -0.5)
    mv = mv.astype(f32).reshape(B, L, M_HEADS, M_DIM)
    gates = jnp.concatenate([mi, mf], axis=-1).astype(f32) + b_gates.astype(f32)
    log_i = gates[..., :M_HEADS]
    log_f = jax.nn.log_sigmoid(gates[..., M_HEADS:])
    aq = rmsnorm(aq.reshape(B, L, A_HEADS, A_DIM), g_q)
    ak = rmsnorm(ak.reshape(B, L, A_KV_HEADS, A_DIM), g_k)
    av = av.reshape(B, L, A_KV_HEADS, A_DIM)
    h_m, attn, state = mixers(mq, mk, mv, log_i, log_f, aq, ak, av, sinks)
    h_m = rmsnorm(h_m, g_mlstm_out) * jax.nn.sigmoid(mo.astype(f32).reshape(B, L, M_HEADS, M_DIM))
    mix = jnp.concatenate([h_m.reshape(B, L, M_W).astype(x.dtype), attn.astype(x.dtype)], axis=-1) @ w_out
    x = x + ga1 * mix
    h2 = rmsnorm(x, g_norm2) * (1 + sc2) + sh2
    u = jnp.square(jax.nn.relu(h2 @ w_up))
    x = x + ga2 * (u @ w_down)
    return x, state


def setup_inputs(seed: int = 0) -> dict:
    key = jax.random.key(seed)
    ks = jax.random.split(key, 24)
    f32 = jnp.float32
    wb = min(WINDOW, PAST_LEN)

    def nrm(k, shape, scale):
        return jax.random.normal(k, shape, f32) * scale

    b_i = nrm(ks[13], (DEPTH, M_HEADS), 0.1)
    b_f = jnp.linspace(3.0, 6.0, M_HEADS, dtype=f32)[None, :] + nrm(ks[14], (DEPTH, M_HEADS), 0.1)
    return {
        'x_prompt': nrm(ks[0], (BATCH, SEQ, D_MODEL), 1.0),
        'x_sample': nrm(ks[1], (DEC_BATCH, DEC_SEQ, D_MODEL), 1.0),
        'c_prompt': nrm(ks[2], (BATCH, D_MODEL), 1.0),
        'c_sample': nrm(ks[3], (DEC_BATCH, D_MODEL), 1.0),
        'cache_swa_k': nrm(ks[4], (DEPTH, DEC_BATCH, wb, A_KV_HEADS, A_DIM), 1.0),
        'cache_swa_v': nrm(ks[5], (DEPTH, DEC_BATCH, wb, A_KV_HEADS, A_DIM), 1.0),
        'state_mlstm_C': nrm(ks[6], (DEPTH, DEC_BATCH, M_HEADS, M_DIM, M_DIM), 0.1),
        'state_mlstm_n': nrm(ks[7], (DEPTH, DEC_BATCH, M_HEADS, M_DIM), 0.1),
        'state_mlstm_m': nrm(ks[8], (DEPTH, DEC_BATCH, M_HEADS), 1.0),
        'w_ada': nrm(ks[9], (DEPTH, D_MODEL, 6 * D_MODEL), 0.5 * D_MODEL ** -0.5),
        'b_ada': nrm(ks[10], (DEPTH, 6 * D_MODEL), 0.02),
        'g_norm1': 1.0 + nrm(ks[11], (DEPTH, D_MODEL), 0.02),
        'w_in': nrm(ks[12], (DEPTH, D_MODEL, N_IN), D_MODEL ** -0.5),
        'b_gates': jnp.concatenate([b_i, b_f], axis=-1),
        'g_q': 1.0 + nrm(ks[15], (DEPTH, A_DIM), 0.02),
        'g_k': 1.0 + nrm(ks[16], (DEPTH, A_DIM), 0.02),
        'sinks': nrm(ks[17], (DEPTH, A_HEADS), 0.5),
        'g_mlstm_out': 1.0 + nrm(ks[18], (DEPTH, M_HEADS, M_DIM), 0.02),
        'w_out': nrm(ks[19], (DEPTH, D_MIX, D_MODEL), D_MIX ** -0.5),
        'g_norm2': 1.0 + nrm(ks[20], (DEPTH, D_MODEL), 0.02),
        'w_up': nrm(ks[21], (DEPTH, D_MODEL, D_FF), D_MODEL ** -0.5),
        'w_down': nrm(ks[22], (DEPTH, D_FF, D_MODEL), D_FF ** -0.5),
    }


def reference(x_prompt, x_sample, c_prompt, c_sample, cache_swa_k, cache_swa_v, state_mlstm_C,
              state_mlstm_n, state_mlstm_m, w_ada, b_ada, g_norm1, w_in, b_gates, g_q, g_k, sinks,
              g_mlstm_out, w_out, g_norm2, w_up, w_down):
    xp, xs = x_prompt, x_sample
    sp, ss = [], []
    for l in range(DEPTH):
        weights = (w_ada[l], b_ada[l], g_norm1[l], w_in[l], b_gates[l], g_q[l], g_k[l], sinks[l],
                   g_mlstm_out[l], w_out[l], g_norm2[l], w_up[l], w_down[l])
        xp, st_p = trunk_layer(xp, c_prompt, prompt_mixers, *weights)
        smix = functools.partial(sample_mixers, cache_k=cache_swa_k[l], cache_v=cache_swa_v[l],
                                 st_c=state_mlstm_C[l], st_n=state_mlstm_n[l], st_m=state_mlstm_m[l])
        xs, st_s = trunk_layer(xs, c_sample, smix, *weights)
        sp.append(st_p)
        ss.append(st_s)

    def stack(sts, i):
        return jnp.stack([s[i] for s in sts], axis=0)

    return (xp, xs,
            stack(sp, 0), stack(sp, 1), stack(sp, 2), stack(sp, 3), stack(sp, 4),
            stack(ss, 0), stack(ss, 1), stack(ss, 2), stack(ss, 3), stack(ss, 4))
```

```python
import numpy as np
from contextlib import ExitStack
import concourse.bass as bass
import concourse.mybir as mybir
from concourse.bass_utils import run_bass_kernel_spmd

F32 = mybir.dt.float32
BF16 = mybir.dt.bfloat16
AF = mybir.ActivationFunctionType
ALU = mybir.AluOpType
AX = mybir.AxisListType

D = 1024
KC = 8
NIN = 2824
DFF = 4096
C_MQ, C_MK, C_MV, C_MO, C_MI, C_MF, C_AQ, C_AK, C_AV = 0, 512, 1024, 1536, 2048, 2052, 2056, 2568, 2696
EPS = 1e-6
PAST = 4096
TS = 32
N_CORES = 8


class Buf:
    __slots__ = ("name", "lw", "rd", "shadow")

    def __init__(self, name, excl=False):
        self.name = name
        self.lw = None
        self.rd = []
        self.shadow = Buf(name + "_x") if excl else None


class Op:
    __slots__ = ("eng", "fn", "deps", "raw", "idx", "sig", "sem", "val", "dma", "ndma", "name")


class Sched:
    ENGS = ("pe", "act", "dve", "pool", "sp")

    DEF_COST = {"pe": 0.4, "act": 0.45, "dve": 0.35, "pool": 0.3, "sp": 0.3}

    def __init__(self, nc):
        self.nc = nc
        self.streams = {e: [] for e in self.ENGS}
        self.all = []
        self.efree = {e: 0.0 for e in self.ENGS}
        self.fin = {}
        self._rec = None

    def begin(self):
        self._stack = getattr(self, "_stack", [])
        self._stack.append(self._rec)
        self._rec = []

    def end(self):
        r = self._rec
        self._rec = self._stack.pop()
        return r

    def put(self, L):
        if self._rec is not None:
            self._rec.extend(L)
        else:
            self.replay(L)

    def interleave(self, M, W):
        out = []
        i = j = 0
        while i < len(M) or j < len(W):
            if j >= len(W) or (i < len(M) and i * len(W) <= j * len(M)):
                out.append(M[i]); i += 1
            else:
                out.append(W[j]); j += 1
        self.put(out)

    def barrier(self, key):
        self._rec.append(("bar", key))

    def mark(self, key):
        self._rec.append(("mark", key))

    def _est_start(self, item):
        eng, fn, r, w, kw = item
        t = self.efree[eng]
        for b in r:
            if b.lw is not None:
                t = max(t, self.fin.get(b.lw, 0.0))
        for b in list(w) + [x.shadow for x in r if x.shadow is not None]:
            if b.lw is not None:
                t = max(t, self.fin.get(b.lw, 0.0))
            for x in b.rd:
                t = max(t, self.fin.get(x, 0.0))
        return t

    def merge(self, A, B, mode="prop", a_first=0):
        if not A:
            return self.replay(B)
        ia = ib = 0
        marks = set()

        def head(L, i):
            while i < len(L) and L[i][0] in ("bar", "mark"):
                if L[i][0] == "mark":
                    marks.add(L[i][1]); i += 1
                elif L[i][1] in marks:
                    i += 1
                else:
                    break
            return i
        while True:
            ia = head(A, ia); ib = head(B, ib)
            ia = head(A, ia)
            ca = A[ia] if ia < len(A) and A[ia][0] not in ("bar",) else None
            cb = B[ib] if ib < len(B) and B[ib][0] not in ("bar",) else None
            if ca is None and cb is None:
                assert ia >= len(A) and ib >= len(B), "merge deadlock on barriers"
                break
            if mode == "prop":
                pick_a = ca is not None and (cb is None or ia < a_first or (ia - a_first) * len(B) <= ib * max(1, len(A) - a_first))
            else:
                pick_a = ca is not None and (cb is None or self._est_start(ca) <= self._est_start(cb))
            if pick_a:
                eng, fn, r, w, kw = ca; ia += 1
            else:
                eng, fn, r, w, kw = cb; ib += 1
            self.op(eng, fn, r, w, **kw)

    def replay(self, L):
        for it in L:
            if it[0] in ("bar", "mark"):
                continue
            eng, fn, r, w, kw = it
            self.op(eng, fn, r, w, **kw)

    limit = None

    def op(self, eng, fn, r=(), w=(), dma=None, ndma=1, name="", force=False, cost=None):
        if self._rec is not None:
            self._rec.append((eng, fn, list(r), list(w), dict(dma=dma, ndma=ndma, name=name, force=force, cost=cost)))
            return None
        if self.limit is not None and len(self.all) >= self.limit and not force:
            return None
        if force:
            r = list(r)
            fin = Buf("fin")
            last = {}
            for x in self.all:
                if x.dma is not None:
                    last[x.dma] = x
            fin_ops = list(last.values())
        else:
            fin_ops = []
        xs_ = [b.shadow for b in r if b.shadow is not None]
        if xs_:
            w = list(w) + xs_
        o = Op()
        o.eng, o.fn, o.dma, o.ndma, o.name = eng, fn, dma, ndma, name
        o.sig = False
        o.sem = None
        o.val = 0
        deps = set()
        for b in r:
            if b.lw is not None:
                deps.add(b.lw)
        o.raw = set(deps)
        for b in w:
            if b.lw is not None:
                deps.add(b.lw)
            for x in b.rd:
                deps.add(x)
        deps.discard(o)
        deps.update(fin_ops)
        o.deps = deps
        for b in r:
            b.rd.append(o)
        for b in w:
            b.lw = o
            b.rd = []
        o.idx = len(self.streams[eng])
        self.streams[eng].append(o)
        self.all.append(o)
        c = cost if cost is not None else (2.5 if dma is not None else self.DEF_COST[eng])
        t0 = self.efree[eng]
        for d in deps:
            t0 = max(t0, self.fin.get(d, 0.0))
        if dma is not None:
            self.efree[eng] = t0 + 0.1
        else:
            self.efree[eng] = t0 + c
        self.fin[o] = t0 + c
        return o

    @staticmethod
    def _needs_wait(o, d):
        if d.dma is not None:
            return True
        if d.eng != o.eng:
            return True
        if o.dma is not None:
            return True
        if o.eng == "pe":
            return False
        return (o.idx - d.idx) <= (8 if d in o.raw else 4)

    def emit(self):
        nc = self.nc
        for o in self.all:
            for d in o.deps:
                if self._needs_wait(o, d):
                    d.sig = True
        with ExitStack() as es:
            esem = {e: es.enter_context(nc.semaphore("s_" + e)) for e in ("pe", "act", "dve", "pool")}
            dsem = {}
            dcnt = {}
            for o in self.all:
                if o.dma is not None and o.dma not in dsem:
                    dsem[o.dma] = es.enter_context(nc.semaphore("d_" + str(o.dma)))
                    dcnt[o.dma] = 0
            for e in ("pe", "act", "dve", "pool"):
                c = 0
                for o in self.streams[e]:
                    if o.dma is None:
                        o.sem = esem[e]
                        if o.sig:
                            c += 1
                            o.val = c
            for o in self.all:
                if o.dma is not None:
                    o.sem = dsem[o.dma]
                    dcnt[o.dma] += 16 * o.ndma
                    o.val = dcnt[o.dma]

            known = {}
            last_on = {e: None for e in self.ENGS}
            for o in self.all:
                kn = {}
                p = last_on[o.eng]
                if p is not None and o.dma is None and p.dma is None:
                    kn.update(known[p])
                for d in o.deps:
                    for k, v in known[d].items():
                        if kn.get(k, 0) < v:
                            kn[k] = v
                    if d.sig or d.dma is not None:
                        if kn.get(d.sem, 0) < d.val:
                            kn[d.sem] = d.val
                if (o.sig or o.dma is not None) and kn.get(o.sem, 0) < o.val:
                    kn[o.sem] = o.val
                known[o] = kn
                if o.dma is None:
                    last_on[o.eng] = o

            def run(ename, eng):
                waited = {}
                for o in self.streams[ename]:
                    need = {}
                    wdeps = [d for d in o.deps if self._needs_wait(o, d)]
                    for d in wdeps:
                        implied = any((d2 is not d) and known[d2].get(d.sem, 0) >= d.val for d2 in wdeps)
                        if implied:
                            continue
                        k = d.sem
                        if need.get(k, (None, 0))[1] < d.val:
                            need[k] = (d.sem, d.val)
                    for k, (s, v) in need.items():
                        if waited.get(k, 0) < v:
                            eng.wait_ge(s, v)
                            waited[k] = v
                    res = o.fn(eng)
                    if o.dma is not None:
                        if not isinstance(res, (list, tuple)):
                            res = [res]
                        assert len(res) == o.ndma, (o.name, len(res), o.ndma)
                        for ins in res:
                            ins.then_inc(o.sem, 16)
                    elif o.sig:
                        res.then_inc(o.sem, 1)

            with nc.Block() as block:
                @block.tensor
                def _(e):
                    run("pe", e)

                @block.scalar
                def _(e):
                    run("act", e)

                @block.vector
                def _(e):
                    run("dve", e)

                @block.gpsimd
                def _(e):
                    run("pool", e)

                @block.sync
                def _(e):
                    run("sp", e)


class Tl:
    __slots__ = ("t", "b")

    def __init__(self, t, name, excl=False):
        self.t = t
        self.b = Buf(name, excl)

    def __getitem__(self, k):
        return self.t[k]


def build_program(NM, NP, phase=99, limit=None, skip_sample=False):
    nc = bass.Bass("TRN2", target_bir_lowering=False)
    es = ExitStack()

    def din(name, shape, dt=F32):
        return nc.dram_tensor(name, list(shape), dt, kind="ExternalInput").ap()

    def dout(name, shape):
        return nc.dram_tensor(name, list(shape), F32, kind="ExternalOutput").ap()

    def dscr(name, shape):
        return nc.dram_tensor(name, list(shape), BF16, kind="Internal").ap()

    xm_d = din("xm", [NM, D]); xp_d = din("xp", [NP, D]); xs_d = din("xs", [TS, D])
    flag_d = din("flag", [128, 1])
    cT_d = din("cT", [128, KC, 2])
    wada_d = din("w_ada", [D, 6 * D]); badaT_d = din("b_adaT", [128, 48]); bada_d = din("b_ada", [1, 6 * D])
    gn1T_d = din("gn1T", [128, KC]); gn2T_d = din("gn2T", [128, KC])
    win_d = din("w_in", [D, NIN]); wout_d = din("w_out", [D, D]); wup_d = din("w_up", [D, DFF]); wdn_d = din("w_down", [DFF, D])
    bg_d = din("bg", [4, 2]); gq_d = din("gq", [128, 1]); gkt_d = din("gkt", [128, 128]); sinks_d = din("sinks_b", [128, 8])
    gmo_d = din("gmo", [128, 4])
    ck_d = din("ck", [128, 128]); cv_d = din("cv", [128, 128])
    stC_d = din("stC", [4, 128, 128]); stnT_d = din("stnT", [128, 4]); stm_d = din("stm", [4, 1])
    ident_d = din("ident", [128, 128]); maskT_d = din("maskT", [128, 128])
    tblA_d = din("tblA", [128, 2, 512]); tblB_d = din("tblB", [128, 2, 512]); diag4_d = din("diag4", [4, 4])

    ym_d = dout("ym", [NM, D]); ys_d = dout("ys", [TS, D])
    okp_d = dout("okp", [128, 128]); ovp_d = dout("ovp", [128, 128]); oCp_d = dout("oCp", [4, 128, 128])
    onp_d = dout("onp", [128, 4]); omp_d = dout("omp", [4, 1])
    oks_d = dout("oks", [128, 128]); ovs_d = dout("ovs", [128, 128]); oCs_d = dout("oCs", [4, 128, 128])
    ons_d = dout("ons", [128, 4]); oms_d = dout("oms", [4, 1])

    wup_s = dscr("wup_s", [D, DFF]); wdn_s = dscr("wdn_s", [DFF, D])

    with es:
        S = Sched(nc)
        S.limit = limit
        cnt = [0]

        def sb(shape, dt, name=None):
            cnt[0] += 1
            name = "sb_" + (name or ("t%d" % cnt[0]))
            return Tl(es.enter_context(nc.sbuf_tensor(name, list(shape), dt)), name)

        banks = [Tl(es.enter_context(nc.psum_tensor("psb%d" % i, [128, 512], F32)), "psb%d" % i, True) for i in range(8)]
        bank_i = [0]
        bank_excl = set()

        bank_ia = [0]

        cur_pool = ["B"]
        bank_im = {"M": 0, "W": 0}

        def nps(pool=None):
            pool = pool or cur_pool[0]
            if pool == "A":
                i = bank_ia[0] % 4
                bank_ia[0] += 1
                return banks[i]
            if pool in ("M", "W"):
                i = (4 if pool == "M" else 6) + bank_im[pool] % 2
                bank_im[pool] += 1
                return banks[i]
            while True:
                i = 4 + bank_i[0] % 4
                bank_i[0] += 1
                if i not in bank_excl:
                    return banks[i]

        dmac = [0]

        def dkey(p):
            dmac[0] += 1
            return "%s%d" % (p, dmac[0])

        dram_out_bufs = []

        def load(eng, dst_tl, dst_ap, src_ap, key=None):
            S.op(eng, lambda e: e.dma_start(out=dst_ap, in_=src_ap), w=[dst_tl.b], dma=key or dkey("l"))

        def store(eng, src_tl, src_ap, dst_ap, key=None):
            ob = Buf("o")
            dram_out_bufs.append(ob)
            S.op(eng, lambda e: e.dma_start(out=dst_ap, in_=src_ap), r=[src_tl.b], w=[ob], dma=key or dkey("s"))

        ident_f = sb([128, 128], F32, "ident_f"); ident_b = sb([128, 128], BF16, "ident_b")
        maskT = sb([128, 128], F32, "maskT")
        tblA = sb([128, 2, 512], BF16, "tblA"); tblB = sb([128, 2, 512], BF16, "tblB")
        diag4 = sb([4, 4], F32, "diag4"); ones4 = sb([4, 128], F32, "ones4"); ones41 = sb([4, 1], F32, "ones41"); ones1 = sb([1, 128], F32, "ones1")
        flag = sb([128, 1], F32, "flag")
        cT = sb([128, KC, 2], F32, "cT"); siluT = sb([128, KC, 2], BF16, "siluT")
        badaT = sb([128, 48], F32, "badaT"); modT = sb([128, 48, 2], F32, "modT")
        gn1T = sb([128, KC], F32, "gn1T"); gn2T = sb([128, KC], F32, "gn2T")
        gam = sb([128, 2, 2, KC], F32, "gam")
        bg = sb([4, 2], F32, "bg"); nbf = sb([4, 1], F32, "nbf")
        gq8 = sb([128, 1], F32, "gq8"); gkt = sb([128, 128], F32, "gkt")
        esink = sb([128, 8], F32, "esink"); gmo = sb([128, 4], F32, "gmo")
        win = sb([128, KC, NIN], BF16, "win"); wout = sb([128, KC, D], BF16, "wout")

        load("sp", ident_f, ident_f[:], ident_d)
        load("sp", maskT, maskT[:], maskT_d)
        load("pool", tblA, tblA[:], tblA_d)
        load("pool", tblB, tblB[:], tblB_d)
        load("sp", diag4, diag4[:], diag4_d)
        load("sp", flag, flag[:], flag_d)
        load("sp", cT, cT[:], cT_d)
        load("sp", badaT, badaT[:], badaT_d)
        load("sp", gn1T, gn1T[:], gn1T_d)
        load("sp", gn2T, gn2T[:], gn2T_d)
        load("sp", bg, bg[:], bg_d)
        load("sp", gq8, gq8[:], gq_d)
        load("sp", gkt, gkt[:], gkt_d)
        load("sp", esink, esink[:], sinks_d)
        load("sp", gmo, gmo[:], gmo_d)
        S.op("dve", lambda e: e.tensor_copy(out=ident_b[:], in_=ident_f[:]), r=[ident_f.b], w=[ident_b.b])
        S.op("dve", lambda e: e.memset(ones4[:], 1.0), w=[ones4.b])
        S.op("dve", lambda e: e.memset(ones41[:], 1.0), w=[ones41.b])
        S.op("dve", lambda e: e.memset(ones1[:], 1.0), w=[ones1.b])
        S.op("dve", lambda e: e.tensor_scalar(out=gq8[:], in0=gq8[:], scalar1=0.125, scalar2=None, op0=ALU.mult), r=[gq8.b], w=[gq8.b])
        S.op("act", lambda e: e.activation(out=esink[:], in_=esink[:], func=AF.Exp), r=[esink.b], w=[esink.b])
        S.op("dve", lambda e: e.tensor_scalar(out=nbf[:], in0=bg[:, 1:2], scalar1=-1.0, scalar2=None, op0=ALU.mult), r=[bg.b], w=[nbf.b])

        NXS = 5
        xslots = [sb([128, D], F32, "x%d" % i) for i in range(NXS)]
        xs_sample = xslots[0]
        ga_bc = [[None, xslots[3]], [sb([128, D], F32, "ga_bc1"), sb([128, D], F32, "ga_bc1s")]]
        xn = sb([128, D], BF16, "xn")
        crep = xn.t[:, :].rearrange("p (c m) -> p c m", m=128)
        st = sb([128, 16], F32, "st")
        hT = sb([128, KC, 512], BF16, "hT")
        mixT = sb([128, KC, 128], BF16, "mixT")
        h2T = sb([128, KC, 512], BF16, "h2T")
        ytmp = [sb([128, 512], F32, "ytmp%d" % i) for i in range(2)]
        qT = sb([128, 4, 512], BF16, "qT"); kT = sb([128, 4, 512], BF16, "kT")
        uT = sb([128, 32, 512], BF16, "uT")
        xn4 = [qT.t[:, 0:2, :].rearrange("p a b -> p (a b)"), qT.t[:, 2:4, :].rearrange("p a b -> p (a b)"),
               kT.t[:, 0:2, :].rearrange("p a b -> p (a b)"), kT.t[:, 2:4, :].rearrange("p a b -> p (a b)")]
        xn4b = [Buf("xn4_%d" % i) for i in range(4)]
        Kp4 = [uT.t[:, i, :] for i in range(4)]
        vext4 = [uT.t[:, 4 + 2 * i:6 + 2 * i, :].rearrange("p a b -> p (a b)")[:, 0:516].rearrange("p (h v) -> p h v", v=129) for i in range(4)]
        Kp4b = [Buf("Kp4_%d" % i) for i in range(4)]
        vext4b = [Buf("vext4_%d" % i) for i in range(4)]
        uTb = [Buf("uT%d" % i) for i in range(32)]
        ga1p = Tl(None, "ga1p")
        ga1p.t = uT.t[:, 0:4, :].rearrange("p a b -> p (a b)").bitcast(F32)
        ga1p.b = Buf("ga1p")
        ga_bc[0][0] = ga1p
        fslots = [sb([128, KC, 512], BF16, "fs%d" % i) for i in range(2)]
        g1 = sb([4, 512], F32, "g1"); g2 = sb([4, 512], F32, "g2"); g3 = sb([4, 512], F32, "g3"); Fg = sb([4, 512], F32, "Fg")
        Fc = sb([4, 1], F32, "Fc"); Mcat = sb([4, 8], F32, "Mcat"); Mc = sb([4, 1], F32, "Mc")
        bmx = sb([4, 4], F32, "bmx"); rho = sb([4, 4], F32, "rho"); rhod = sb([4, 4, 4], F32, "rhod")
        rho_bc = sb([128, 4, 4], F32, "rho_bc")
        eb = [sb([128, 8], F32, "eb%d" % i) for i in range(4)]
        Kp = sb([128, 512], BF16, "Kp")
        vext = sb([128, 4, 129], BF16, "vext")
        Eo = sb([128, 512], F32, "Eo")
        qn = sb([128, 512], BF16, "qn")
        knf = sb([128, 128], F32, "knf"); vf = sb([128, 128], F32, "vf")
        knb2 = sb([128, 2, 2, 64], BF16, "knb2")
        vsw = [sb([128, 2, 65], BF16, "vsw%d" % i) for i in range(2)]
        kTs = [sb([128, 2, 128], BF16, "kTs%d" % i) for i in range(2)]
        qTs = sb([128, 4, 128], BF16, "qTs")
        STt = sb([128, 4, 128], BF16, "STt")
        Cbf = sb([128, 4, 129], BF16, "Cbf"); Chat = sb([128, 4, 129], F32, "Chat")
        Pm = [sb([128, 512], BF16, "Pm%d" % i) for i in range(4)]
        mix = sb([128, D], BF16, "mix")
        rtmp = sb([128, 512], F32, "rtmp")
        sqt = rtmp
        junk_ap = rtmp.t[:, :].bitcast(BF16)
        nvec = sb([128, 4], F32, "nvec")
        junk2 = sb([128, D], BF16, "junk2")
        st3 = sb([128, 16], F32, "st3")
        st4 = st3
        mixWb = Buf("mixW")
        mo = sb([4, 1], F32, "mo")

        S.op("dve", lambda e: e.memset(vext[:], 1.0), w=[vext.b])
        for i in range(2):
            S.op("dve", lambda e, i=i: e.memset(vsw[i][:], 1.0), w=[vsw[i].b])

        sil_t = sb([128, KC, 2], F32, "sil_t")
        S.op("act", lambda e: e.activation(out=sil_t[:], in_=cT[:], func=AF.Exp, scale=-1.0), r=[cT.b], w=[sil_t.b])
        S.op("dve", lambda e: e.tensor_scalar(out=sil_t[:], in0=sil_t[:], scalar1=1.0, scalar2=None, op0=ALU.add), r=[sil_t.b], w=[sil_t.b])
        S.op("dve", lambda e: e.reciprocal(out=sil_t[:], in_=sil_t[:]), r=[sil_t.b], w=[sil_t.b])
        S.op("dve", lambda e: e.tensor_tensor(out=siluT[:], in0=sil_t[:], in1=cT[:], op=ALU.mult), r=[sil_t.b, cT.b], w=[siluT.b])

        stg = []
        for i in range(2):
            tl = Tl(None, "stg%d" % i)
            tl.t = uT.t[:, 8 + 12 * i:20 + 12 * i, :].rearrange("p a b -> p (a b)").bitcast(F32)
            tl.b = Buf("stg%d" % i)
            stg.append(tl)
        for kc in range(KC):
            sg = stg[kc % 2]
            load("sp", sg, sg[:, 0:NIN], win_d[kc * 128:(kc + 1) * 128, :], key="stg%d" % (kc % 2))
            S.op("dve", lambda e, kc=kc, sg=sg: e.tensor_copy(out=win[:, kc, :], in_=sg[:, 0:NIN]), r=[sg.b], w=[win.b], cost=1.7)
        for kc2 in range(KC // 2):
            sg = stg[kc2 % 2]
            load("sp", sg, sg[:, 0:2 * D].rearrange("p (c n) -> p c n", n=D), wout_d[kc2 * 256:(kc2 + 1) * 256, :].rearrange("(c p) n -> p c n", p=128), key="stg%d" % (kc2 % 2))
            S.op("dve", lambda e, kc2=kc2, sg=sg: e.tensor_copy(out=wout[:, 2 * kc2:2 * kc2 + 2, :], in_=sg[:, 0:2 * D].rearrange("p (c n) -> p c n", n=D)),
                 r=[sg.b], w=[wout.b], cost=1.3)
        mod_ps = nps()
        bank_excl.add(4)
        for j in range(12):
            sl = fslots[j % 2]
            load("pool", sl, sl[:], wada_d[:, j * 512:(j + 1) * 512].rearrange("(c p) n -> p c n", p=128), key="wa%d" % (j % 2))
            seg = j // 2
            if seg in (2, 5):
                gi = 0 if seg == 2 else 1
                col = (j % 2) * 512
                load("sp", rtmp, rtmp[0:1, :], bada_d[:, j * 512:(j + 1) * 512])
                for r in range(2):
                    S.op("dve", lambda e, r=r: e.tensor_copy(out=crep, in_=siluT[:, :, r:r + 1].to_broadcast([128, KC, 128])),
                         r=[siluT.b], w=[xn.b])
                    pb = nps()

                    def mmg(e, sl=sl, pb=pb):
                        i = None
                        for kc in range(KC):
                            i = e.matmul(pb[:, :], lhsT=crep[:, kc, :], rhs=sl[:, kc, :], start=(kc == 0), stop=False)
                        i = e.matmul(pb[:, :], lhsT=ones1[0:1, :], rhs=rtmp[0:1, :], start=False, stop=True)
                        return i
                    S.op("pe", mmg, r=[xn.b, sl.b, rtmp.b, ones1.b], w=[pb.b])
                    if gi == 0 and r == 1:
                        tgt, tcol = ytmp[j % 2], 0
                    else:
                        tgt, tcol = ga_bc[gi][r], col
                    S.op("act", lambda e, tgt=tgt, pb=pb, tcol=tcol: e.activation(out=tgt[:, tcol:tcol + 512], in_=pb[:, :], func=AF.Copy),
                         r=[pb.b], w=[tgt.b])
            else:
                def mmf(e, sl=sl, j=j):
                    i = None
                    for q in range(4):
                        nchunk = j * 4 + q
                        for kc in range(KC):
                            i = e.matmul(mod_ps[:, nchunk * 2:nchunk * 2 + 2], lhsT=sl[:, kc, q * 128:(q + 1) * 128], rhs=siluT[:, kc, :],
                                         start=(kc == 0), stop=(kc == KC - 1))
                    return i
                S.op("pe", mmf, r=[sl.b, siluT.b], w=[mod_ps.b])
        for j0 in (0, 24):
            S.op("dve", lambda e, j0=j0: e.tensor_tensor(out=modT[:, j0:j0 + 16, :], in0=mod_ps[:, 2 * j0:2 * j0 + 32].rearrange("p (j r) -> p j r", r=2),
                                                         in1=badaT[:, j0:j0 + 16].unsqueeze(2).to_broadcast([128, 16, 2]), op=ALU.add),
                 r=[mod_ps.b, badaT.b], w=[modT.b])
        for n, (gT, sc0) in enumerate(((gn1T, 8), (gn2T, 32))):
            for r in range(2):
                S.op("dve", lambda e, n=n, r=r, gT=gT, sc0=sc0: e.scalar_tensor_tensor(
                    out=gam[:, n, r, :], in0=modT[:, sc0:sc0 + 8, r], scalar=1.0, in1=gT[:], op0=ALU.add, op1=ALU.mult),
                    r=[modT.b, gT.b], w=[gam.b])
        bank_excl.discard(4)

        def sh_ap(n, r, c):
            base = 0 if n == 0 else 24
            return modT[:, base + c, r:r + 1]

        for h in range(4):
            S.op("dve", lambda e, h=h: e.tensor_scalar(out=wout[:, h, :], in0=wout[:, h, :], scalar1=gmo[:, h:h + 1], scalar2=None, op0=ALU.mult),
                 r=[wout.b, gmo.b], w=[wout.b])
        scrb = Buf("scr")
        def issue_scratch_casts():
            for q in range(4):
                S.op("pool", lambda e, q=q: e.dma_start(out=wup_s[q * 256:(q + 1) * 256, :], in_=wup_d[q * 256:(q + 1) * 256, :]), w=[scrb], dma="scr")
            for q in range(4):
                S.op("pool", lambda e, q=q: e.dma_start(out=wdn_s[q * 1024:(q + 1) * 1024, :], in_=wdn_d[q * 1024:(q + 1) * 1024, :]), w=[scrb], dma="scr")


        def rstd_from_ss(ss_ap, out_ap, n, tl):
            S.op("act", lambda e: e.activation(out=out_ap, in_=ss_ap, func=AF.Ln, scale=1.0 / n, bias=EPS), r=[tl.b], w=[tl.b])
            S.op("act", lambda e: e.activation(out=out_ap, in_=out_ap, func=AF.Exp, scale=-0.5), r=[tl.b], w=[tl.b])

        def norm_T(xt, T, dstT, col0, n, r):
            S.op("act", lambda e: e.activation(out=junk2[0:T, :], in_=xt[0:T, :], func=AF.Square, accum_out=st[0:T, 0:1]),
                 r=[xt.b], w=[st.b], cost=1.1)
            rstd_from_ss(st[0:T, 0:1], st[0:T, 1:2], D, st)
            S.op("dve", lambda e: e.tensor_scalar(out=xn[0:T, :], in0=xt[0:T, :], scalar1=st[0:T, 1:2], scalar2=None, op0=ALU.mult),
                 r=[xt.b, st.b], w=[xn.b], cost=0.7)
            pb = nps()
            pbb = pb[:, :].bitcast(BF16)

            def tr(e):
                i = None
                for c in range(KC):
                    i = e.transpose(out=pbb[:, c * 128:c * 128 + T], in_=xn[0:T, c * 128:(c + 1) * 128], identity=ident_b[0:T, 0:T])
                return i
            S.op("pe", tr, r=[xn.b, ident_b.b], w=[pb.b], cost=0.6)
            base = 0 if n == 0 else 24
            pv3 = pbb[:, :].rearrange("p (c t) -> p c t", t=128)[:, :, 0:T]
            S.op("dve", lambda e: e.tensor_tensor(out=dstT[:, :, col0:col0 + T], in0=pv3, in1=gam[:, n, r, :].unsqueeze(2).to_broadcast([128, KC, T]), op=ALU.mult),
                 r=[pb.b, gam.b], w=[dstT.b], cost=1.2)
            S.op("dve", lambda e: e.tensor_tensor(out=dstT[:, :, col0:col0 + T], in0=dstT[:, :, col0:col0 + T],
                                                  in1=modT[:, base:base + KC, r].unsqueeze(2).to_broadcast([128, KC, T]), op=ALU.add),
                 r=[dstT.b, modT.b], w=[dstT.b], cost=0.7)

        def norm_T4(xts, dstT, n, r):
            T = 128
            for b in range(4):
                S.op("act", lambda e, b=b: e.activation(out=junk2[:, :], in_=xts[b][:, :], func=AF.Square, accum_out=st4[:, b:b + 1]),
                     r=[xts[b].b], w=[st4.b], cost=1.1)
            rstd_from_ss(st4[:, 0:4], st4[:, 4:8], D, st4)
            for b in range(4):
                S.op("act", lambda e, b=b: e.activation(out=xn4[b], in_=xts[b][:, :], func=AF.Identity, scale=st4[:, 4 + b:5 + b]),
                     r=[xts[b].b, st4.b], w=[xn4b[b], (qT.b if b < 2 else kT.b)], cost=1.1)
            base = 0 if n == 0 else 24
            for pair in range(2):
                pbs_ = {}
                for b in (2 * pair, 2 * pair + 1):
                    pbs_[b] = nps()
                    pbb = pbs_[b][:, :].bitcast(BF16)

                    def tr(e, b=b, pbb=pbb):
                        i = None
                        for c in range(KC):
                            i = e.transpose(out=pbb[:, c * 128:(c + 1) * 128], in_=xn4[b][:, c * 128:(c + 1) * 128], identity=ident_b[:, :])
                        return i
                    S.op("pe", tr, r=[xn4b[b], ident_b.b], w=[pbs_[b].b], cost=0.6)
                for b in (2 * pair, 2 * pair + 1):
                    pv3 = pbs_[b][:, :].bitcast(BF16).rearrange("p (c t) -> p c t", t=128)
                    S.op("dve", lambda e, b=b, pv3=pv3: e.tensor_tensor(out=dstT[:, :, b * 128:(b + 1) * 128], in0=pv3,
                                                                       in1=gam[:, n, r, :].unsqueeze(2).to_broadcast([128, KC, 128]), op=ALU.mult),
                         r=[pbs_[b].b, gam.b], w=[dstT.b], cost=1.2)
                    S.op("dve", lambda e, b=b: e.tensor_tensor(out=dstT[:, :, b * 128:(b + 1) * 128], in0=dstT[:, :, b * 128:(b + 1) * 128],
                                                               in1=modT[:, base:base + KC, r].unsqueeze(2).to_broadcast([128, KC, 128]), op=ALU.add),
                         r=[dstT.b, modT.b], w=[dstT.b], cost=0.7)

        def gates(NT, T, NB, hsrc=None, defer_tail=False):
            hsrc = hT if hsrc is None else hsrc
            Fin_ap = Fc[:, 0:1]
            Min_ap = Mc[:, 0:1]
            pi = nps(); pf = nps()

            def mg(e, p, c0):
                i = None
                for kc in range(KC):
                    i = e.matmul(p[0:4, 0:NT], lhsT=win[:, kc, c0:c0 + 4], rhs=hsrc[:, kc, 0:NT], start=(kc == 0), stop=(kc == KC - 1))
                return i
            S.op("pe", lambda e: mg(e, pi, C_MI), r=[win.b, hsrc.b], w=[pi.b])
            S.op("pe", lambda e: mg(e, pf, C_MF), r=[win.b, hsrc.b], w=[pf.b])
            S.op("act", lambda e: e.activation(out=g1[:, 0:NT], in_=pi[0:4, 0:NT], func=AF.Identity, bias=bg[:, 0:1]), r=[pi.b, bg.b], w=[g1.b])
            S.op("act", lambda e: e.activation(out=g2[:, 0:NT], in_=pf[0:4, 0:NT], func=AF.Exp, scale=-1.0, bias=nbf[:, 0:1]), r=[pf.b, nbf.b], w=[g2.b])
            S.op("act", lambda e: e.activation(out=g2[:, 0:NT], in_=g2[:, 0:NT], func=AF.Ln, bias=1.0), r=[g2.b], w=[g2.b])
            S.op("dve", lambda e: e.tensor_tensor_scan(out=Fg[:, 0:NT], data0=ones4[:, 0:1].to_broadcast([4, NT]), data1=g2[:, 0:NT],
                                                       initial=Fin_ap, op0=ALU.mult, op1=ALU.subtract),
                 r=[ones4.b, g2.b, Fc.b], w=[Fg.b])
            S.op("dve", lambda e: e.tensor_tensor(out=g1[:, 0:NT], in0=g1[:, 0:NT], in1=Fg[:, 0:NT], op=ALU.subtract), r=[g1.b, Fg.b], w=[g1.b])
            S.op("dve", lambda e: e.tensor_reduce(out=bmx[:, 0:NB], in_=g1[:, 0:NT].rearrange("p (b t) -> p b t", t=T), axis=AX.X, op=ALU.max),
                 r=[g1.b], w=[bmx.b])
            S.op("dve", lambda e: e.tensor_copy(out=Mcat[:, 0:1], in_=Min_ap), r=[Mc.b], w=[Mcat.b])
            S.op("dve", lambda e: e.tensor_tensor_scan(out=Mcat[:, 1:1 + NB], data0=bmx[:, 0:NB], data1=bmx[:, 0:NB], initial=Min_ap,
                                                       op0=ALU.max, op1=ALU.max),
                 r=[bmx.b, Mc.b], w=[Mcat.b])
            S.op("dve", lambda e: e.tensor_tensor(out=rho[:, 0:NB], in0=Mcat[:, 0:NB], in1=Mcat[:, 1:1 + NB], op=ALU.subtract), r=[Mcat.b], w=[rho.b])
            S.op("act", lambda e: e.activation(out=rho[:, 0:NB], in_=rho[:, 0:NB], func=AF.Exp), r=[rho.b], w=[rho.b])
            mb = Mcat[:, 1:1 + NB].unsqueeze(2).to_broadcast([4, NB, T])
            S.op("dve", lambda e: e.tensor_tensor(out=g2[:, 0:NT].rearrange("p (b t) -> p b t", t=T), in0=g1[:, 0:NT].rearrange("p (b t) -> p b t", t=T),
                                                  in1=mb, op=ALU.subtract), r=[g1.b, Mcat.b], w=[g2.b])
            S.op("act", lambda e: e.activation(out=g2[:, 0:NT], in_=g2[:, 0:NT], func=AF.Exp, bias=float(np.log(128.0 ** -0.5))), r=[g2.b], w=[g2.b])
            S.op("dve", lambda e: e.tensor_tensor(out=g3[:, 0:NT].rearrange("p (b t) -> p b t", t=T), in0=Fg[:, 0:NT].rearrange("p (b t) -> p b t", t=T),
                                                  in1=mb, op=ALU.add), r=[Fg.b, Mcat.b], w=[g3.b])
            S.op("act", lambda e: e.activation(out=g3[:, 0:NT], in_=g3[:, 0:NT], func=AF.Exp, scale=-1.0), r=[g3.b], w=[g3.b])
            if defer_tail:
                return lambda: gates_tail(NT, T, NB)
            gates_tail(NT, T, NB)

        def gates_tail(NT, T, NB):
            for b in range(NB):
                pt = nps()

                def trg(e, b=b, pt=pt):
                    e.transpose(out=pt[0:T, 0:4], in_=g2[:, b * T:(b + 1) * T], identity=ident_f[0:4, 0:4])
                    return e.transpose(out=pt[0:T, 4:8], in_=g3[:, b * T:(b + 1) * T], identity=ident_f[0:4, 0:4])
                S.op("pe", trg, r=[g2.b, g3.b, ident_f.b], w=[pt.b])
                S.op("dve", lambda e, b=b, pt=pt: e.tensor_copy(out=eb[b][0:T, :], in_=pt[0:T, 0:8]), r=[pt.b], w=[eb[b].b])
            S.op("dve", lambda e: e.tensor_tensor(out=rhod[:, 0:NB, :], in0=rho[:, 0:NB].unsqueeze(2).to_broadcast([4, NB, 4]),
                                                  in1=diag4[:].unsqueeze(1).to_broadcast([4, NB, 4]), op=ALU.mult),
                 r=[rho.b, diag4.b], w=[rhod.b])
            pr = nps()
            S.op("pe", lambda e: e.matmul(pr[:, 0:NB * 4], lhsT=ones4[:, :], rhs=rhod[:, 0:NB, :].rearrange("p b h -> p (b h)"), start=True, stop=True),
                 r=[ones4.b, rhod.b], w=[pr.b])
            S.op("dve", lambda e: e.tensor_copy(out=rho_bc[:, 0:NB, :].rearrange("p b h -> p (b h)"), in_=pr[:, 0:NB * 4]), r=[pr.b], w=[rho_bc.b])
            S.op("dve", lambda e: e.tensor_copy(out=Fc[:], in_=Fg[:, NT - 1:NT]), r=[Fg.b], w=[Fc.b])
            S.op("dve", lambda e: e.tensor_copy(out=Mc[:], in_=Mcat[:, NB:NB + 1]), r=[Mcat.b], w=[Mc.b])

        def tokmm(T, col_tok0, c0, ncols, pb, hsrc=None):
            hsrc = hT if hsrc is None else hsrc

            def f(e):
                i = None
                for kc in range(KC):
                    i = e.matmul(pb[0:T, 0:ncols], lhsT=hsrc[:, kc, col_tok0:col_tok0 + T], rhs=win[:, kc, c0:c0 + ncols],
                                 start=(kc == 0), stop=(kc == KC - 1))
                return i
            S.op("pe", f, r=[hsrc.b, win.b], w=[pb.b], cost=8 * max(ncols, 64) / 2200.0)

        def kv_state(T, b, col_tok0):
            pk = nps()
            tokmm(T, col_tok0, C_MK, 512, pk)
            for h in range(4):
                if h < 2:
                    S.op("act", lambda e, h=h: e.activation(out=Kp[0:T, h * 128:(h + 1) * 128], in_=pk[0:T, h * 128:(h + 1) * 128],
                                                           func=AF.Identity, scale=eb[b][0:T, h:h + 1]),
                         r=[pk.b, eb[b].b], w=[Kp.b])
                else:
                    S.op("dve", lambda e, h=h: e.tensor_scalar(out=Kp[0:T, h * 128:(h + 1) * 128], in0=pk[0:T, h * 128:(h + 1) * 128],
                                                                scalar1=eb[b][0:T, h:h + 1], scalar2=None, op0=ALU.mult),
                         r=[pk.b, eb[b].b], w=[Kp.b])
            pv = nps()
            tokmm(T, col_tok0, C_MV, 512, pv)
            S.op("act", lambda e: e.activation(out=vext[0:T, :, 0:128], in_=pv[0:T, :].rearrange("p (h v) -> p h v", v=128), func=AF.Copy),
                 r=[pv.b], w=[vext.b])

        def state_update(T, b, Kp_=None, Kpb=None, vext_=None, vextb=None):
            Kp_ = Kp.t if Kp_ is None else Kp_
            Kpb = Kp.b if Kpb is None else Kpb
            vext_ = vext.t if vext_ is None else vext_
            vextb = vext.b if vextb is None else vextb
            for hp in range(2):
                pc = nps()

                def f(e, hp=hp, pc=pc):
                    i = None
                    for hh in range(2):
                        h = hp * 2 + hh
                        i = e.matmul(pc[:, hh * 129:(hh + 1) * 129], lhsT=Kp_[0:T, h * 128:(h + 1) * 128], rhs=vext_[0:T, h, :], start=True, stop=True)
                    return i
                S.op("pe", f, r=[Kpb, vextb], w=[pc.b])
                for hh in range(2):
                    h = hp * 2 + hh
                    S.op("dve", lambda e, h=h, hh=hh, pc=pc: e.scalar_tensor_tensor(
                        out=Chat[:, h, :], in0=Chat[:, h, :], scalar=rho_bc[:, b, h:h + 1], in1=pc[:, hh * 129:(hh + 1) * 129],
                        op0=ALU.mult, op1=ALU.add), r=[Chat.b, rho_bc.b, pc.b], w=[Chat.b])

        def swa_kv(T, col_tok0, cur, want_f32, hsrc=None):
            pkv = nps()
            tokmm(T, col_tok0, C_AK, 256, pkv, hsrc)
            S.op("act", lambda e: e.activation(out=sqt[0:T, 0:128], in_=pkv[0:T, 0:128], func=AF.Square), r=[pkv.b], w=[sqt.b])
            S.op("dve", lambda e: e.tensor_reduce(out=st[0:T, 2:4], in_=sqt[0:T, 0:128].rearrange("p (k d) -> p k d", d=64), axis=AX.X, op=ALU.add),
                 r=[sqt.b], w=[st.b])
            rstd_from_ss(st[0:T, 2:4], st[0:T, 4:6], 64, st)
            S.op("dve", lambda e: e.tensor_tensor(out=knf[0:T, :].rearrange("p (k d) -> p k d", d=64), in0=pkv[0:T, 0:128].rearrange("p (k d) -> p k d", d=64),
                                                  in1=st[0:T, 4:6].unsqueeze(2).to_broadcast([T, 2, 64]), op=ALU.mult),
                 r=[pkv.b, st.b], w=[knf.b])
            S.op("dve", lambda e: e.tensor_tensor(out=knf[0:T, :], in0=knf[0:T, :], in1=gkt[0:T, :], op=ALU.mult), r=[knf.b, gkt.b], w=[knf.b])
            S.op("dve", lambda e: e.tensor_copy(out=knb2[0:T, :, :, :], in_=knf[0:T, :].rearrange("p (k d) -> p k d", d=64).unsqueeze(2).to_broadcast([T, 2, 2, 64])),
                 r=[knf.b], w=[knb2.b])
            S.op("act", lambda e: e.activation(out=vsw[cur][0:T, :, 0:64], in_=pkv[0:T, 128:256].rearrange("p (k d) -> p k d", d=64), func=AF.Copy),
                 r=[pkv.b], w=[vsw[cur].b])
            if want_f32:
                S.op("dve", lambda e: e.tensor_copy(out=vf[0:T, :], in_=pkv[0:T, 128:256]), r=[pkv.b], w=[vf.b])
            kt_from_knb2(T, cur)

        def kt_from_knb2(T, cur):
            pt = nps()
            ptb = pt[:, :].bitcast(BF16)

            def f(e):
                i = None
                for kv in range(2):
                    i = e.transpose(out=ptb[:, kv * 128:kv * 128 + T], in_=knb2[0:T, kv, :, :].rearrange("p a d -> p (a d)"), identity=ident_b[0:T, 0:T])
                return i
            S.op("pe", f, r=[knb2.b, ident_b.b], w=[pt.b])
            S.op("act", lambda e: e.activation(out=kTs[cur][:, :, 0:T], in_=ptb[:, 0:256].rearrange("p (k t) -> p k t", t=128)[:, :, 0:T], func=AF.Copy),
                 r=[pt.b], w=[kTs[cur].b])

        def mixer_block(T, b, col_tok0, cur, TkA, r, want_f32, cbf_ready=False, cbf_next=False):
            prev = 1 - cur
            S.begin()
            cur_pool[0] = "M"
            ps_ = nps()

            def mms(e):
                i = None
                for h in range(4):
                    i = e.matmul(ps_[0:T, h * 128:h * 128 + T], lhsT=kT[:, h, col_tok0:col_tok0 + T], rhs=qT[:, h, col_tok0:col_tok0 + T], start=True, stop=True)
                return i
            S.op("pe", mms, r=[kT.b, qT.b], w=[ps_.b])
            for h in range(4):
                S.op("dve", lambda e, h=h: e.scalar_tensor_tensor(out=STt[0:T, h, 0:T], in0=ps_[0:T, h * 128:h * 128 + T], scalar=eb[b][0:T, h:h + 1],
                                                                   in1=maskT[0:T, 0:T], op0=ALU.mult, op1=ALU.mult),
                     r=[ps_.b, eb[b].b, maskT.b], w=[STt.b])
            kv_state(T, b, col_tok0)
            po = nps()
            tokmm(T, col_tok0, C_MO, 512, po)
            S.op("act", lambda e: e.activation(out=Eo[0:T, :], in_=po[0:T, :], func=AF.Exp, scale=-1.0), r=[po.b], w=[Eo.b])
            M1 = S.end()
            S.begin()
            cur_pool[0] = "W"
            pq = nps()
            tokmm(T, col_tok0, C_AQ, 512, pq)
            S.op("act", lambda e: e.activation(out=sqt[0:T, :], in_=pq[0:T, :], func=AF.Square), r=[pq.b], w=[sqt.b])
            S.op("dve", lambda e: e.tensor_reduce(out=st[0:T, 8:16], in_=sqt[0:T, :].rearrange("p (k d) -> p k d", d=64), axis=AX.X, op=ALU.add),
                 r=[sqt.b], w=[st.b])
            rstd_from_ss(st[0:T, 8:16], st[0:T, 8:16], 64, st)
            S.op("dve", lambda e: e.tensor_tensor(out=qn[0:T, :].rearrange("p (k d) -> p k d", d=64), in0=pq[0:T, :].rearrange("p (k d) -> p k d", d=64),
                                                  in1=st[0:T, 8:16].unsqueeze(2).to_broadcast([T, 8, 64]), op=ALU.mult),
                 r=[pq.b, st.b], w=[qn.b])
            pt = nps()
            ptb = pt[:, :].bitcast(BF16)

            def trq(e):
                i = None
                for p in range(4):
                    i = e.transpose(out=ptb[:, p * 128:p * 128 + T], in_=qn[0:T, p * 128:(p + 1) * 128], identity=ident_b[0:T, 0:T])
                return i
            S.op("pe", trq, r=[qn.b, ident_b.b], w=[pt.b])
            S.op("dve", lambda e: e.tensor_scalar(out=qTs[:, :, 0:T], in0=ptb[:, 0:512].rearrange("p (k t) -> p k t", t=128)[:, :, 0:T],
                                                  scalar1=gq8[:, 0:1], scalar2=None, op0=ALU.mult), r=[pt.b, gq8.b], w=[qTs.b])
            swa_kv(T, col_tok0, cur, want_f32)
            W1 = S.end()
            S.begin()
            cur_pool[0] = "M"
            def emit_cbf(bb):
                for h in range(4):
                    S.op("act", lambda e, h=h, bb=bb: e.activation(out=Cbf[:, h, :], in_=Chat[:, h, :], func=AF.Identity, scale=rho_bc[:, bb, h:h + 1]),
                         r=[Chat.b, rho_bc.b], w=[Cbf.b])
            if not cbf_ready:
                emit_cbf(b)
            pn = [nps(), nps()]
            for hp in range(2):
                def mmn(e, hp=hp):
                    i = None
                    for hh in range(2):
                        h = hp * 2 + hh
                        e.matmul(pn[hp][0:T, hh * 129:(hh + 1) * 129], lhsT=STt[0:T, h, 0:T], rhs=vext[0:T, h, :], start=True, stop=False)
                        i = e.matmul(pn[hp][0:T, hh * 129:(hh + 1) * 129], lhsT=qT[:, h, col_tok0:col_tok0 + T], rhs=Cbf[:, h, :], start=False, stop=True)
                    return i
                S.op("pe", mmn, r=[STt.b, vext.b, qT.b, Cbf.b], w=[pn[hp].b])
            for hp in range(2):
                S.op("dve", lambda e, hp=hp: e.tensor_copy(out=st2[0:T, hp * 2:hp * 2 + 2], in_=pn[hp][0:T, 0:258].rearrange("p (h v) -> p h v", v=129)[:, :, 128]),
                     r=[pn[hp].b], w=[st2.b])
            S.op("dve", lambda e: e.tensor_scalar(out=st2[0:T, 4:8], in0=st2[0:T, 0:4], scalar1=-1.0, scalar2=None, op0=ALU.mult), r=[st2.b], w=[st2.b])
            S.op("dve", lambda e: e.tensor_tensor(out=st2[0:T, 0:4], in0=st2[0:T, 0:4], in1=st2[0:T, 4:8], op=ALU.max), r=[st2.b], w=[st2.b])
            S.op("dve", lambda e: e.tensor_tensor(out=st2[0:T, 0:4], in0=st2[0:T, 0:4], in1=eb[b][0:T, 4:8], op=ALU.max), r=[st2.b, eb[b].b], w=[st2.b])
            S.op("dve", lambda e: e.reciprocal(out=st2[0:T, 0:4], in_=st2[0:T, 0:4]), r=[st2.b], w=[st2.b])
            for h in range(4):
                hp, hh = divmod(h, 2)
                S.op("act", lambda e, h=h, hp=hp, hh=hh: e.activation(out=junk2[0:T, h * 128:(h + 1) * 128], in_=pn[hp][0:T, hh * 129:hh * 129 + 128],
                                                                     func=AF.Square, accum_out=st2[0:T, 8 + h:9 + h]),
                     r=[pn[hp].b], w=[st2.b])
            S.op("dve", lambda e: e.tensor_tensor(out=st2[0:T, 8:12], in0=st2[0:T, 8:12], in1=st2[0:T, 0:4], op=ALU.mult), r=[st2.b], w=[st2.b])
            S.op("dve", lambda e: e.tensor_tensor(out=st2[0:T, 8:12], in0=st2[0:T, 8:12], in1=st2[0:T, 0:4], op=ALU.mult), r=[st2.b], w=[st2.b])
            S.op("act", lambda e: e.activation(out=st2[0:T, 8:12], in_=st2[0:T, 8:12], func=AF.Ln, scale=1.0 / 128, bias=EPS), r=[st2.b], w=[st2.b])
            S.op("act", lambda e: e.activation(out=st2[0:T, 8:12], in_=st2[0:T, 8:12], func=AF.Exp, scale=0.5), r=[st2.b], w=[st2.b])
            S.op("dve", lambda e: e.reciprocal(out=st2[0:T, 4:8], in_=st2[0:T, 0:4]), r=[st2.b], w=[st2.b])
            S.op("dve", lambda e: e.tensor_tensor(out=st2[0:T, 8:12], in0=st2[0:T, 8:12], in1=st2[0:T, 4:8], op=ALU.mult), r=[st2.b], w=[st2.b])
            S.op("dve", lambda e: e.tensor_scalar(out=Eo[0:T, :], in0=Eo[0:T, :], scalar1=1.0, scalar2=None, op0=ALU.add), r=[Eo.b], w=[Eo.b])
            S.op("dve", lambda e: e.tensor_tensor(out=Eo[0:T, :].rearrange("p (h v) -> p h v", v=128), in0=Eo[0:T, :].rearrange("p (h v) -> p h v", v=128),
                                                  in1=st2[0:T, 8:12].unsqueeze(2).to_broadcast([T, 4, 128]), op=ALU.mult), r=[Eo.b, st2.b], w=[Eo.b])
            S.op("dve", lambda e: e.reciprocal(out=Eo[0:T, :], in_=Eo[0:T, :]), r=[Eo.b], w=[Eo.b])
            for hp in range(2):
                S.op("dve", lambda e, hp=hp: e.tensor_tensor(out=mix[0:T, hp * 256:(hp + 1) * 256].rearrange("p (h v) -> p h v", v=128),
                                                             in0=pn[hp][0:T, 0:258].rearrange("p (h v) -> p h v", v=129)[:, :, 0:128],
                                                             in1=Eo[0:T, hp * 256:(hp + 1) * 256].rearrange("p (h v) -> p h v", v=128), op=ALU.mult),
                     r=[pn[hp].b, Eo.b], w=[mix.b])
            state_update(T, b)
            if cbf_next:
                emit_cbf(b + 1)
            M2 = S.end()
            S.begin()
            cur_pool[0] = "W"
            tiles = [(prev, TkA, tblA), (cur, T, tblB)]
            for kv in range(2):
                for ti, (kb, Tk, tbl) in enumerate(tiles):
                    pls = [nps(), nps()]

                    def mml(e, kv=kv, kb=kb, Tk=Tk, pls=pls):
                        i = None
                        for par in range(2):
                            for a in range(2):
                                g = 2 * a + par
                                hd = kv * 4 + g
                                p, odd = divmod(hd, 2)
                                assert odd == par
                                r0 = 64 * odd
                                i = e.matmul(pls[par][0:Tk, a * 128:a * 128 + T], lhsT=kTs[kb][r0:r0 + 64, kv, 0:Tk], rhs=qTs[r0:r0 + 64, p, 0:T], start=True, stop=True)
                        return i
                    S.op("pe", mml, r=[kTs[kb].b, qTs.b], w=[pls[0].b, pls[1].b])
                    P = Pm[kv * 2 + ti]
                    for par in range(2):
                        S.op("act", lambda e, P=P, pls=pls, Tk=Tk, par=par: e.activation(
                            out=P[0:Tk, :].rearrange("p (a b t) -> p a b t", b=2, t=128)[:, :, par, 0:T],
                            in_=pls[par][0:Tk, 0:256].rearrange("p (a t) -> p a t", t=128)[:, :, 0:T], func=AF.Exp),
                            r=[pls[par].b], w=[P.b])
                    S.op("dve", lambda e, P=P, tbl=tbl, kv=kv, Tk=Tk: e.tensor_tensor(
                        out=P[0:Tk, :].rearrange("p (g t) -> p g t", t=128)[:, :, 0:T], in0=P[0:Tk, :].rearrange("p (g t) -> p g t", t=128)[:, :, 0:T],
                        in1=tbl[0:Tk, kv, :].rearrange("p (g t) -> p g t", t=128)[:, :, 0:T], op=ALU.mult), r=[P.b, tbl.b], w=[P.b])
            for kv in range(2):
                pa = nps()

                def mmpv(e, kv=kv, pa=pa):
                    i = None
                    for g in range(4):
                        for ti, (kb, Tk, tbl) in enumerate(tiles):
                            P = Pm[kv * 2 + ti]
                            i = e.matmul(pa[0:T, g * 65:(g + 1) * 65], lhsT=P[0:Tk, g * 128:g * 128 + T], rhs=vsw[kb][0:Tk, kv, :], start=(ti == 0), stop=(ti == 1))
                    return i
                S.op("pe", mmpv, r=[Pm[kv * 2].b, Pm[kv * 2 + 1].b, vsw[0].b, vsw[1].b], w=[pa.b])
                pav = pa[0:T, 0:260].rearrange("p (g v) -> p g v", v=65)
                S.op("dve", lambda e, pav=pav, kv=kv: e.tensor_tensor(out=st3[0:T, 12:16], in0=pav[:, :, 64], in1=esink[0:T, kv * 4:(kv + 1) * 4], op=ALU.add),
                     r=[pa.b, esink.b], w=[st3.b])
                S.op("dve", lambda e: e.reciprocal(out=st3[0:T, 12:16], in_=st3[0:T, 12:16]), r=[st3.b], w=[st3.b])
                S.op("dve", lambda e, pav=pav, kv=kv: e.tensor_tensor(out=mix[0:T, 512 + kv * 256:512 + (kv + 1) * 256].rearrange("p (g d) -> p g d", d=64),
                                                                      in0=pav[:, :, 0:64], in1=st3[0:T, 12:16].unsqueeze(2).to_broadcast([T, 4, 64]), op=ALU.mult),
                     r=[pa.b, st3.b], w=[mixWb])
            W2 = S.end()
            cur_pool[0] = "B"
            S.interleave(M1 + M2, W1 + W2)
            pm_ = nps()
            pmb = pm_[:, :].bitcast(BF16)

            def trm(e):
                i = None
                for c in range(KC):
                    i = e.transpose(out=pmb[:, c * 128:c * 128 + T], in_=mix[0:T, c * 128:(c + 1) * 128], identity=ident_b[0:T, 0:T])
                return i
            S.op("pe", trm, r=[mix.b, mixWb, ident_b.b], w=[pm_.b])
            S.op("act", lambda e: e.activation(out=mixT[:, :, 0:T], in_=pmb[:, :].rearrange("p (c t) -> p c t", t=128)[:, :, 0:T], func=AF.Copy),
                 r=[pm_.b], w=[mixT.b])

        st2 = sb([128, 16], F32, "st2")

        def wout_res(T, col_tok0, xt, r, folded):
            for n in range(2):
                pb = nps()

                def f(e, n=n, pb=pb):
                    i = None
                    for kc in range(KC):
                        i = e.matmul(pb[0:T, :], lhsT=mixT[:, kc, 0:T], rhs=wout[:, kc, n * 512:(n + 1) * 512], start=(kc == 0), stop=(kc == KC - 1))
                    return i
                S.op("pe", f, r=[mixT.b, wout.b], w=[pb.b], cost=8 * 0.23)
                if folded:
                    S.op("dve", lambda e, n=n, pb=pb: e.tensor_tensor(out=xt[0:T, n * 512:(n + 1) * 512], in0=xt[0:T, n * 512:(n + 1) * 512], in1=pb[0:T, :], op=ALU.add),
                         r=[xt.b, pb.b], w=[xt.b], cost=0.7)
                else:
                    S.op("dve", lambda e, n=n, pb=pb: e.tensor_tensor(out=rtmp[0:T, :], in0=pb[0:T, :], in1=ytmp[n][0:T, :], op=ALU.mult),
                         r=[pb.b, ytmp[n].b], w=[rtmp.b])
                    S.op("dve", lambda e, n=n: e.tensor_tensor(out=xt[0:T, n * 512:(n + 1) * 512], in0=xt[0:T, n * 512:(n + 1) * 512], in1=rtmp[0:T, :], op=ALU.add),
                         r=[xt.b, rtmp.b], w=[xt.b])

        def fold_ga1():
            for kc in range(KC):
                S.op("dve", lambda e, kc=kc: e.tensor_tensor(out=wout[:, kc, :], in0=wout[:, kc, :], in1=ga_bc[0][0][:, :], op=ALU.mult),
                     r=[wout.b, ga_bc[0][0].b], w=[wout.b])

        yk = [0]

        def ffn_up(NT, h2src=None):
            h2src = h2T if h2src is None else h2src
            for g in range(8):
                sl = fslots[g % 2]
                S.op("sp", lambda e, sl=sl, g=g: e.dma_start(out=sl[:], in_=wup_s[:, g * 512:(g + 1) * 512].rearrange("(c p) n -> p c n", p=128)),
                     r=[scrb], w=[sl.b], dma="fs%d" % (g % 2), cost=3.0)
                for j in range(4):
                    pb = nps("A")

                    for half in range(2):
                        def f(e, sl=sl, j=j, pb=pb, half=half):
                            i = None
                            for kc in range(half * 4, half * 4 + 4):
                                i = e.matmul(pb[:, 0:NT], lhsT=sl[:, kc, j * 128:(j + 1) * 128], rhs=h2src[:, kc, 0:NT], start=(kc == 0), stop=(kc == KC - 1))
                            return i
                        S.op("pe", f, r=[sl.b, h2src.b], w=[pb.b], cost=4 * max(NT, 64) / 2200.0)
                    ub = uTb[g * 4 + j]
                    S.op("act", lambda e, g=g, j=j, pb=pb: e.activation(out=uT[:, g * 4 + j, 0:NT], in_=pb[:, 0:NT], func=AF.Relu), r=[pb.b], w=[ub] + ([ga1p.b] if g * 4 + j < 4 else []), cost=0.2 + NT / 1200.0)
                    S.op("dve", lambda e, g=g, j=j: e.tensor_tensor(out=uT[:, g * 4 + j, 0:NT], in0=uT[:, g * 4 + j, 0:NT], in1=uT[:, g * 4 + j, 0:NT], op=ALU.mult),
                         r=[ub], w=[ub], cost=0.1 + NT / 1900.0)

        def ffn_down(blocks, r):
            for n in range(2):
                pbs = [nps("A") for _ in blocks]
                for g in range(4):
                    sl = fslots[g % 2]
                    S.op("sp", lambda e, sl=sl, g=g, n=n: e.dma_start(out=sl[:], in_=wdn_s[g * 1024:(g + 1) * 1024, n * 512:(n + 1) * 512].rearrange("(c p) n -> p c n", p=128)),
                         r=[scrb], w=[sl.b], dma="fs%d" % (g % 2), cost=3.0)
                    for bi, (yd, T, c0, yb) in enumerate(blocks):
                        for half in range(2):
                            def f(e, sl=sl, g=g, bi=bi, T=T, c0=c0, pbs=pbs, half=half):
                                i = None
                                for j in range(half * 4, half * 4 + 4):
                                    i = e.matmul(pbs[bi][0:T, :], lhsT=uT[:, g * 8 + j, c0:c0 + T], rhs=sl[:, j, :], start=(g == 0 and j == 0), stop=(g == 3 and j == 7))
                                return i
                            S.op("pe", f, r=uTb[g * 8 + half * 4:g * 8 + half * 4 + 4] + [sl.b], w=[pbs[bi].b], cost=4 * 0.23)
                for bi, (yd, T, c0, yb) in enumerate(blocks):
                    yt = ytmp[yk[0] % 2]
                    yk[0] += 1
                    S.op("dve", lambda e, bi=bi, T=T, n=n, pbs=pbs, yt=yt: e.tensor_tensor(out=yt[0:T, :], in0=pbs[bi][0:T, :], in1=ga_bc[1][r][0:T, n * 512:(n + 1) * 512], op=ALU.mult),
                         r=[pbs[bi].b, ga_bc[1][r].b], w=[yt.b], cost=0.7)
                    S.op("pool", lambda e, yd=yd, T=T, n=n, yt=yt: e.dma_start(out=yd[:, n * 512:(n + 1) * 512], in_=yt[0:T, :], accum_op=ALU.add),
                         r=[yt.b], w=[yb], dma="ya_" + yt.b.name, cost=3.0)

        def qk_feature_major(NT):
            for which, c0, dst in ((0, C_MQ, qT), (1, C_MK, kT)):
                for h in range(4):
                    pb = nps()

                    def f(e, c0=c0, h=h, pb=pb):
                        i = None
                        for kc in range(KC):
                            i = e.matmul(pb[:, 0:NT], lhsT=win[:, kc, c0 + h * 128:c0 + (h + 1) * 128], rhs=hT[:, kc, 0:NT], start=(kc == 0), stop=(kc == KC - 1))
                        return i
                    S.op("pe", f, r=[win.b, hT.b], w=[pb.b], cost=8 * max(NT, 64) / 2200.0)
                    S.op("act", lambda e, dst=dst, h=h, pb=pb: e.activation(out=dst[:, h, 0:NT], in_=pb[:, 0:NT], func=AF.Copy), r=[pb.b],
                         w=[dst.b] + (xn4b[0:2] if which == 0 else xn4b[2:4]), cost=0.2 + NT / 1200.0)

        def write_state(oC, on, om, ok, ov, T_k):
            store("pool", Chat, Chat[:, :, 0:128], oC.rearrange("h d v -> d h v"))
            S.op("dve", lambda e: e.tensor_copy(out=nvec[:], in_=Chat[:, :, 128]), r=[Chat.b], w=[nvec.b])
            store("pool", nvec, nvec[:], on)
            S.op("dve", lambda e: e.tensor_tensor(out=mo[:], in0=Fc[:], in1=Mc[:], op=ALU.add), r=[Fc.b, Mc.b], w=[mo.b])
            store("pool", mo, mo[:], om)

        xi = [0]

        xorder = [1, 2, 4, 3, 0]

        def next_x(src_d, row0):
            s = xslots[xorder[xi[0] % NXS]]
            xi[0] += 1
            load("sp", s, s[:], src_d[row0:row0 + 128, :], key="x_" + s.b.name)
            return s

        pre_x0 = [None]
        T = TS
        if phase < 1:
            S.op("pool", lambda e: e.nop(), r=dram_out_bufs, force=True)
            S.emit()
            return nc
        if not skip_sample:
            load("sp", xs_sample, xs_sample[0:T, :], xs_d)
            load("sp", Chat, Chat[:, :, 0:128], stC_d.rearrange("h d v -> d h v"))
            load("sp", nvec, nvec[:], stnT_d)
            S.op("dve", lambda e: e.tensor_copy(out=Chat[:, :, 128], in_=nvec[:]), r=[nvec.b], w=[Chat.b])
            load("sp", Mc, Mc[:], stm_d)
            S.op("dve", lambda e: e.memset(Fc[:], 0.0), w=[Fc.b])
            load("sp", knf, knf[:], ck_d)
            load("sp", vf, vf[:], cv_d)
            S.op("dve", lambda e: e.tensor_copy(out=knb2[:, :, :, :], in_=knf[:, :].rearrange("p (k d) -> p k d", d=64).unsqueeze(2).to_broadcast([128, 2, 2, 64])),
                 r=[knf.b], w=[knb2.b])
            kt_from_knb2(128, 1)
            S.op("dve", lambda e: e.tensor_copy(out=vsw[1][:, :, 0:64], in_=vf[:, :].rearrange("p (k d) -> p k d", d=64)), r=[vf.b], w=[vsw[1].b])
            for (src, dst) in ((ck_d, oks_d), (cv_d, ovs_d)):
                ob = Buf("o"); dram_out_bufs.append(ob)
                S.op("pool", lambda e, src=src, dst=dst: e.dma_start(out=dst[0:128 - TS, :], in_=src[TS:128, :]), w=[ob], dma=dkey("s"))
            if NP >= 512:
                pre_x0[0] = [next_x(xp_d, b * 128) for b in range(4)]
            norm_T(xs_sample, T, hT, 0, 0, 1)
            qk_feature_major(T)
            gates(T, T, 1)
            mixer_block(T, 0, 0, 0, 128, 1, True)
            store("pool", knf, knf[0:T, :], oks_d[128 - TS:128, :])
            store("pool", vf, vf[0:T, :], ovs_d[128 - TS:128, :])
            write_state(oCs_d, ons_d, oms_d, None, None, None)
            wout_res(T, 0, xs_sample, 1, False)
            h2s = Tl(None, "h2s")
            h2s.t = xn.t[:, :].rearrange("p (c t) -> p c t", t=128)
            h2s.b = Buf("h2s")
            norm_T(xs_sample, T, h2s, 0, 1, 1)
            ysb = Buf("ys"); dram_out_bufs.append(ysb)
            S.op("pool", lambda e: e.dma_start(out=ys_d, in_=xs_sample[0:TS, :]), r=[xs_sample.b], w=[ysb], dma=dkey("s"))
            fold_ga1()
            S.begin()
            ffn_up(T, h2s)
            S.mark("up-1")
            ffn_down([(ys_d, T, 0, ysb)], 1)
            sample_ffn = S.end()
        else:
            fold_ga1()
            sample_ffn = []

        issue_scratch_casts()
        if phase < 2:
            S.replay(sample_ffn)
            S.op("pool", lambda e: e.nop(), r=dram_out_bufs, force=True)
            S.emit()
            return nc
        T = 128
        S.op("dve", lambda e: e.memset(Chat[:], 0.0), w=[Chat.b])
        S.op("dve", lambda e: e.memset(Fc[:], 0.0), w=[Fc.b])
        S.op("dve", lambda e: e.memset(Mc[:], 0.0), w=[Mc.b])
        n_pt = NP // 512
        if n_pt > 0:
            for b in range(4):
                S.op("dve", lambda e, b=b: e.memset(vext4[b], 1.0), w=[vext4b[b], Kp4b[b], ga1p.b])
        hbuf = [hT, h2T]

        def pre_N(t):
            S.begin()
            cur_pool[0] = "A"
            if t == 0 and pre_x0[0] is not None:
                xt = pre_x0[0]
            else:
                xt = [next_x(xp_d, t * 512 + b * 128) for b in range(4)]
            norm_T4(xt, hbuf[t % 2], 0, 0)
            cur_pool[0] = "B"
            return S.end()

        def pre_K(t):
            S.begin()
            cur_pool[0] = "B"
            hs = hbuf[t % 2]
            gtail = gates(512, T, 4, hs, defer_tail=True)
            for b in range(4):
                pv = nps()
                tokmm(T, b * 128, C_MV, 512, pv, hs)
                S.op("act", lambda e, b=b, pv=pv: e.activation(out=vext4[b][:, :, 0:128], in_=pv[:, :].rearrange("p (h v) -> p h v", v=128), func=AF.Copy),
                     r=[pv.b], w=[vext4b[b]], cost=0.65)
            gtail()
            for b in range(4):
                pk = nps()
                tokmm(T, b * 128, C_MK, 512, pk, hs)
                for h in range(4):
                    if h < 2:
                        S.op("act", lambda e, h=h, b=b, pk=pk: e.activation(out=Kp4[b][:, h * 128:(h + 1) * 128], in_=pk[:, h * 128:(h + 1) * 128],
                                                                           func=AF.Identity, scale=eb[b][:, h:h + 1]),
                             r=[pk.b, eb[b].b], w=[Kp4b[b]])
                    else:
                        S.op("dve", lambda e, h=h, b=b, pk=pk: e.tensor_scalar(out=Kp4[b][:, h * 128:(h + 1) * 128], in0=pk[:, h * 128:(h + 1) * 128],
                                                                             scalar1=eb[b][:, h:h + 1], scalar2=None, op0=ALU.mult),
                             r=[pk.b, eb[b].b], w=[Kp4b[b]])
            for b in range(4):
                state_update(T, b, Kp4[b], Kp4b[b], vext4[b], vext4b[b])
            if t == n_pt - 1:
                swa_kv(T, 3 * 128, 1, False, hs)
            cur_pool[0] = "B"
            return S.end()

        if n_pt > 0:
            S.replay(pre_N(0))
        for t in range(n_pt):
            Kl = pre_K(t)
            if t + 1 < n_pt:
                S.merge(pre_N(t + 1), Kl, a_first=20)
            else:
                S.replay(Kl)
        if n_pt > 0:
            S.op("dve", lambda e: e.memset(st3[:, 8:9], 0.0), r=Kp4b + vext4b, w=[st3.b, ga1p.b] + uTb[0:12])
        if n_pt > 0:
            S.op("dve", lambda e: e.tensor_scalar(out=Chat[:], in0=Chat[:], scalar1=flag[:, 0:1], scalar2=None, op0=ALU.mult), r=[Chat.b, flag.b], w=[Chat.b])
            S.op("dve", lambda e: e.tensor_scalar(out=Fc[:], in0=Fc[:], scalar1=flag[0:4, 0:1], scalar2=None, op0=ALU.mult), r=[Fc.b, flag.b], w=[Fc.b])
            S.op("dve", lambda e: e.tensor_scalar(out=Mc[:], in0=Mc[:], scalar1=flag[0:4, 0:1], scalar2=None, op0=ALU.mult), r=[Mc.b, flag.b], w=[Mc.b])
            S.op("dve", lambda e: e.tensor_scalar(out=vsw[1][:], in0=vsw[1][:], scalar1=flag[:, 0:1], scalar2=None, op0=ALU.mult), r=[vsw[1].b, flag.b], w=[vsw[1].b])
        else:
            S.op("dve", lambda e: e.memset(vsw[1][:], 0.0), w=[vsw[1].b])
            S.op("dve", lambda e: e.memset(kTs[1][:], 0.0), w=[kTs[1].b])

        if phase < 3:
            S.replay(sample_ffn)
            S.op("pool", lambda e: e.nop(), r=dram_out_bufs, force=True)
            S.emit()
            return nc
        n_mt = NM // 512
        curh = [0]

        def thread_B(t):
            S.begin()
            xt = xt_next[0]
            gtail = gates(512, T, 4, defer_tail=True)
            qk_feature_major(512)
            gtail()
            for b in range(4):
                last = (t == n_mt - 1 and b == 3)
                cur = curh[0]
                mixer_block(T, b, b * 128, cur, 128, 0, last, cbf_ready=(b > 0), cbf_next=(b < 3))
                if last:
                    store("pool", knf, knf[:, :], okp_d)
                    store("pool", vf, vf[:, :], ovp_d)
                wout_res(T, b * 128, xt[b], 0, True)
                curh[0] = 1 - cur
                if t == 0 and b == 0:
                    S.op("dve", lambda e, c=curh[0]: e.memset(vsw[c][:, :, 64:65], 1.0), w=[vsw[curh[0]].b])
            S.barrier("up%d" % (t - 1))
            blocks = []
            norm_T4(xt, h2T, 1, 0)
            for b in range(4):
                yd = ym_d[t * 512 + b * 128:t * 512 + (b + 1) * 128, :]
                yb = Buf("y"); dram_out_bufs.append(yb)
                S.op("sp", lambda e, yd=yd, xb=xt[b]: e.dma_start(out=yd, in_=xb[:, :]), r=[xt[b].b], w=[yb], dma="y_" + xt[b].b.name)
                blocks.append((yd, T, b * 128, yb))
            if t + 1 < n_mt:
                xt_next[0] = [next_x(xm_d, (t + 1) * 512 + b * 128) for b in range(4)]
                norm_T4(xt_next[0], hT, 0, 0)
            return S.end(), blocks

        def thread_A(t, blocks):
            S.begin()
            ffn_up(512)
            S.mark("up%d" % t)
            ffn_down(blocks, 0)
            return S.end()

        xt_next = [None]
        if n_mt > 0:
            xt_next[0] = [next_x(xm_d, b * 128) for b in range(4)]
            norm_T4(xt_next[0], hT, 0, 0)
            Bl, blocks = thread_B(0)
            S.merge(sample_ffn, Bl, a_first=14)
            for t in range(n_mt):
                Al = thread_A(t, blocks)
                if t + 1 < n_mt:
                    Bl, blocks = thread_B(t + 1)
                    S.merge(Al, Bl, a_first=24)
                else:
                    S.replay(Al)
        write_state(oCp_d, onp_d, omp_d, None, None, None)
        S.op("pool", lambda e: e.nop(), r=dram_out_bufs, force=True)
        S.op("sp", lambda e: e.nop(), r=dram_out_bufs, force=True)
        S.emit()
    return nc


_PROG = {}


def _tables():
    slopes = np.exp2(-8.0 * np.arange(1, 9, dtype=np.float64) / 8).astype(np.float64)
    s = np.arange(128)[:, None]
    t = np.arange(128)[None, :]
    distA = (t + 128 - s).astype(np.float64)
    allowA = (((s - 128) // 64) >= (t // 64) - 2)
    distB = np.abs(t - s).astype(np.float64)
    allowB = ((s // 64) <= (t // 64))
    tA = np.zeros((128, 2, 4, 128), np.float32)
    tB = np.zeros((128, 2, 4, 128), np.float32)
    for kv in range(2):
        for g in range(4):
            sl = slopes[kv * 4 + g]
            tA[:, kv, g, :] = np.exp(-sl * distA) * allowA
            tB[:, kv, g, :] = np.exp(-sl * distB) * allowB
    maskT = (s <= t).astype(np.float32)
    return tA.reshape(128, 2, 512), tB.reshape(128, 2, 512), maskT


def _prep_inputs(inp, NM, NP, n_seq_halves=2):
    f = lambda a: np.ascontiguousarray(a, dtype=np.float32)
    xp = inp["x_prompt"]
    Bn, L, _ = xp.shape
    tA, tB, maskT = _tables()
    common = {
        "w_ada": f(inp["w_ada"][0]), "b_adaT": f(inp["b_ada"][0].reshape(48, 128).T), "b_ada": f(inp["b_ada"][0].reshape(1, -1)),
        "gn1T": f(inp["g_norm1"][0].reshape(KC, 128).T), "gn2T": f(inp["g_norm2"][0].reshape(KC, 128).T),
        "w_in": f(inp["w_in"][0]), "w_out": f(inp["w_out"][0]), "w_up": f(inp["w_up"][0]), "w_down": f(inp["w_down"][0]),
        "bg": f(inp["b_gates"][0].reshape(2, 4).T),
        "gq": f(np.tile(inp["g_q"][0], 2).reshape(128, 1)),
        "gkt": f(np.tile(inp["g_k"][0][None, :], (128, 2))),
        "sinks_b": f(np.tile(inp["sinks"][0][None, :], (128, 1))),
        "gmo": f(inp["g_mlstm_out"][0].T),
        "ident": np.eye(128, dtype=np.float32), "maskT": maskT, "tblA": tA, "tblB": tB, "diag4": np.eye(4, dtype=np.float32),
    }
    maps = []
    for core in range(N_CORES):
        bi, half = divmod(core, 2)
        m = dict(common)
        m["xm"] = f(xp[bi, half * NM:(half + 1) * NM])
        if half == 0:
            m["xp"] = np.zeros((NP, D), np.float32)
        else:
            m["xp"] = f(xp[bi, 0:NP])
        m["flag"] = np.full((128, 1), float(half), np.float32)
        m["xs"] = f(inp["x_sample"][core])
        c2 = np.stack([inp["c_prompt"][bi], inp["c_sample"][core]], axis=0)
        m["cT"] = f(c2.reshape(2, KC, 128).transpose(2, 1, 0))
        m["ck"] = f(inp["cache_swa_k"][0, core].reshape(128, 128))
        m["cv"] = f(inp["cache_swa_v"][0, core].reshape(128, 128))
        m["stC"] = f(inp["state_mlstm_C"][0, core])
        m["stnT"] = f(inp["state_mlstm_n"][0, core].T)
        m["stm"] = f(inp["state_mlstm_m"][0, core].reshape(4, 1))
        maps.append(m)
    return maps


def kernel(**inputs):
    inp = {k: np.asarray(v) for k, v in inputs.items()}
    Bn, L, _ = inp["x_prompt"].shape
    NM = L // 2
    NP = NM
    key = (NM, NP)
    if key not in _PROG:
        _PROG[key] = build_program(NM, NP)
    nc = _PROG[key]
    maps = _prep_inputs(inp, NM, NP)
    res = run_bass_kernel_spmd(nc, maps, core_ids=list(range(N_CORES)))
    R = res.results
    y_p = np.zeros((Bn, L, D), np.float32)
    for core in range(N_CORES):
        bi, half = divmod(core, 2)
        y_p[bi, half * NM:(half + 1) * NM] = R[core]["ym"]
    y_s = np.stack([R[c]["ys"] for c in range(N_CORES)], axis=0)
    last = [2 * b + 1 for b in range(Bn)]
    swa_k_p = np.stack([R[c]["okp"].reshape(128, 2, 64) for c in last], 0)[None]
    swa_v_p = np.stack([R[c]["ovp"].reshape(128, 2, 64) for c in last], 0)[None]
    C_p = np.stack([R[c]["oCp"] for c in last], 0)[None]
    n_p = np.stack([R[c]["onp"].T for c in last], 0)[None]
    m_p = np.stack([R[c]["omp"].reshape(4) for c in last], 0)[None]
    allc = list(range(N_CORES))
    swa_k_s = np.stack([R[c]["oks"].reshape(128, 2, 64) for c in allc], 0)[None]
    swa_v_s = np.stack([R[c]["ovs"].reshape(128, 2, 64) for c in allc], 0)[None]
    C_s = np.stack([R[c]["oCs"] for c in allc], 0)[None]
    n_s = np.stack([R[c]["ons"].T for c in allc], 0)[None]
    m_s = np.stack([R[c]["oms"].reshape(4) for c in allc], 0)[None]
    outs = (y_p, y_s, swa_k_p, swa_v_p, C_p, n_p, m_p, swa_k_s, swa_v_s, C_s, n_s, m_s)
    return tuple(np.ascontiguousarray(o, dtype=np.float32) for o in outs)
```

```python
import numpy as np
from contextlib import ExitStack
import concourse.bass as bass
import concourse.mybir as mybir
from concourse.bass_utils import run_bass_kernel_spmd

F32 = mybir.dt.float32
BF16 = mybir.dt.bfloat16
AF = mybir.ActivationFunctionType
ALU = mybir.AluOpType
AX = mybir.AxisListType

D = 1024
KC = 8
NIN = 2824
DFF = 4096
C_MQ, C_MK, C_MV, C_MO, C_MI, C_MF, C_AQ, C_AK, C_AV = 0, 512, 1024, 1536, 2048, 2052, 2056, 2568, 2696
EPS = 1e-6
PAST = 4096
TS = 32
N_CORES = 8


class Buf:
    __slots__ = ("name", "lw", "rd", "shadow")

    def __init__(self, name, excl=False):
        self.name = name
        self.lw = None
        self.rd = []
        self.shadow = Buf(name + "_x") if excl else None


class Op:
    __slots__ = ("eng", "fn", "deps", "raw", "idx", "sig", "sem", "val", "dma", "ndma", "name")


class Sched:
    ENGS = ("pe", "act", "dve", "pool", "sp")

    DEF_COST = {"pe": 0.4, "act": 0.45, "dve": 0.35, "pool": 0.3, "sp": 0.3}

    def __init__(self, nc):
        self.nc = nc
        self.streams = {e: [] for e in self.ENGS}
        self.all = []
        self.efree = {e: 0.0 for e in self.ENGS}
        self.fin = {}
        self._rec = None

    def begin(self):
        self._stack = getattr(self, "_stack", [])
        self._stack.append(self._rec)
        self._rec = []

    def end(self):
        r = self._rec
        self._rec = self._stack.pop()
        return r

    def put(self, L):
        if self._rec is not None:
            self._rec.extend(L)
        else:
            self.replay(L)

    def interleave(self, M, W):
        out = []
        i = j = 0
        while i < len(M) or j < len(W):
            if j >= len(W) or (i < len(M) and i * len(W) <= j * len(M)):
                out.append(M[i]); i += 1
            else:
                out.append(W[j]); j += 1
        self.put(out)

    def barrier(self, key):
        self._rec.append(("bar", key))

    def mark(self, key):
        self._rec.append(("mark", key))

    def _est_start(self, item):
        eng, fn, r, w, kw = item
        t = self.efree[eng]
        for b in r:
            if b.lw is not None:
                t = max(t, self.fin.get(b.lw, 0.0))
        for b in list(w) + [x.shadow for x in r if x.shadow is not None]:
            if b.lw is not None:
                t = max(t, self.fin.get(b.lw, 0.0))
            for x in b.rd:
                t = max(t, self.fin.get(x, 0.0))
        return t

    def merge(self, A, B, mode="prop", a_first=0):
        if not A:
            return self.replay(B)
        ia = ib = 0
        marks = set()

        def head(L, i):
            while i < len(L) and L[i][0] in ("bar", "mark"):
                if L[i][0] == "mark":
                    marks.add(L[i][1]); i += 1
                elif L[i][1] in marks:
                    i += 1
                else:
                    break
            return i
        while True:
            ia = head(A, ia); ib = head(B, ib)
            ia = head(A, ia)
            ca = A[ia] if ia < len(A) and A[ia][0] not in ("bar",) else None
            cb = B[ib] if ib < len(B) and B[ib][0] not in ("bar",) else None
            if ca is None and cb is None:
                assert ia >= len(A) and ib >= len(B), "merge deadlock on barriers"
                break
            if mode == "prop":
                pick_a = ca is not None and (cb is None or ia < a_first or (ia - a_first) * len(B) <= ib * max(1, len(A) - a_first))
            else:
                pick_a = ca is not None and (cb is None or self._est_start(ca) <= self._est_start(cb))
            if pick_a:
                eng, fn, r, w, kw = ca; ia += 1
            else:
                eng, fn, r, w, kw = cb; ib += 1
            self.op(eng, fn, r, w, **kw)

    def replay(self, L):
        for it in L:
            if it[0] in ("bar", "mark"):
                continue
            eng, fn, r, w, kw = it
            self.op(eng, fn, r, w, **kw)

    limit = None

    def op(self, eng, fn, r=(), w=(), dma=None, ndma=1, name="", force=False, cost=None):
        if self._rec is not None:
            self._rec.append((eng, fn, list(r), list(w), dict(dma=dma, ndma=ndma, name=name, force=force, cost=cost)))
            return None
        if self.limit is not None and len(self.all) >= self.limit and not force:
            return None
        if force:
            r = list(r)
            fin = Buf("fin")
            last = {}
            for x in self.all:
                if x.dma is not None:
                    last[x.dma] = x
            fin_ops = list(last.values())
        else:
            fin_ops = []
        xs_ = [b.shadow for b in r if b.shadow is not None]
        if xs_:
            w = list(w) + xs_
        o = Op()
        o.eng, o.fn, o.dma, o.ndma, o.name = eng, fn, dma, ndma, name
        o.sig = False
        o.sem = None
        o.val = 0
        deps = set()
        for b in r:
            if b.lw is not None:
                deps.add(b.lw)
        o.raw = set(deps)
        for b in w:
            if b.lw is not None:
                deps.add(b.lw)
            for x in b.rd:
                deps.add(x)
        deps.discard(o)
        deps.update(fin_ops)
        o.deps = deps
        for b in r:
            b.rd.append(o)
        for b in w:
            b.lw = o
            b.rd = []
        o.idx = len(self.streams[eng])
        self.streams[eng].append(o)
        self.all.append(o)
        c = cost if cost is not None else (2.5 if dma is not None else self.DEF_COST[eng])
        t0 = self.efree[eng]
        for d in deps:
            t0 = max(t0, self.fin.get(d, 0.0))
        if dma is not None:
            self.efree[eng] = t0 + 0.1
        else:
            self.efree[eng] = t0 + c
        self.fin[o] = t0 + c
        return o

    @staticmethod
    def _needs_wait(o, d):
        if d.dma is not None:
            return True
        if d.eng != o.eng:
            return True
        if o.dma is not None:
            return True
        if o.eng == "pe":
            return False
        return (o.idx - d.idx) <= (6 if d in o.raw else 3)

    def emit(self):
        nc = self.nc
        for o in self.all:
            for d in o.deps:
                if self._needs_wait(o, d):
                    d.sig = True
        with ExitStack() as es:
            esem = {e: es.enter_context(nc.semaphore("s_" + e)) for e in ("pe", "act", "dve", "pool")}
            dsem = {}
            dcnt = {}
            for o in self.all:
                if o.dma is not None and o.dma not in dsem:
                    dsem[o.dma] = es.enter_context(nc.semaphore("d_" + str(o.dma)))
                    dcnt[o.dma] = 0
            for e in ("pe", "act", "dve", "pool"):
                c = 0
                for o in self.streams[e]:
                    if o.dma is None:
                        o.sem = esem[e]
                        if o.sig:
                            c += 1
                            o.val = c
            for o in self.all:
                if o.dma is not None:
                    o.sem = dsem[o.dma]
                    dcnt[o.dma] += 16 * o.ndma
                    o.val = dcnt[o.dma]

            known = {}
            last_on = {e: None for e in self.ENGS}
            for o in self.all:
                kn = {}
                p = last_on[o.eng]
                if p is not None and o.dma is None and p.dma is None:
                    kn.update(known[p])
                for d in o.deps:
                    for k, v in known[d].items():
                        if kn.get(k, 0) < v:
                            kn[k] = v
                    if d.sig or d.dma is not None:
                        if kn.get(d.sem, 0) < d.val:
                            kn[d.sem] = d.val
                if (o.sig or o.dma is not None) and kn.get(o.sem, 0) < o.val:
                    kn[o.sem] = o.val
                known[o] = kn
                if o.dma is None:
                    last_on[o.eng] = o

            def run(ename, eng):
                waited = {}
                for o in self.streams[ename]:
                    need = {}
                    wdeps = [d for d in o.deps if self._needs_wait(o, d)]
                    for d in wdeps:
                        implied = any((d2 is not d) and known[d2].get(d.sem, 0) >= d.val for d2 in wdeps)
                        if implied:
                            continue
                        k = d.sem
                        if need.get(k, (None, 0))[1] < d.val:
                            need[k] = (d.sem, d.val)
                    for k, (s, v) in need.items():
                        if waited.get(k, 0) < v:
                            eng.wait_ge(s, v)
                            waited[k] = v
                    res = o.fn(eng)
                    if o.dma is not None:
                        if not isinstance(res, (list, tuple)):
                            res = [res]
                        assert len(res) == o.ndma, (o.name, len(res), o.ndma)
                        for ins in res:
                            ins.then_inc(o.sem, 16)
                    elif o.sig:
                        res.then_inc(o.sem, 1)

            with nc.Block() as block:
                @block.tensor
                def _(e):
                    run("pe", e)

                @block.scalar
                def _(e):
                    run("act", e)

                @block.vector
                def _(e):
                    run("dve", e)

                @block.gpsimd
                def _(e):
                    run("pool", e)

                @block.sync
                def _(e):
                    run("sp", e)


class Tl:
    __slots__ = ("t", "b")

    def __init__(self, t, name, excl=False):
        self.t = t
        self.b = Buf(name, excl)

    def __getitem__(self, k):
        return self.t[k]


def build_program(NM, NP, phase=99, limit=None, skip_sample=False):
    nc = bass.Bass("TRN2", target_bir_lowering=False)
    es = ExitStack()

    def din(name, shape, dt=F32):
        return nc.dram_tensor(name, list(shape), dt, kind="ExternalInput").ap()

    def dout(name, shape):
        return nc.dram_tensor(name, list(shape), F32, kind="ExternalOutput").ap()

    def dscr(name, shape):
        return nc.dram_tensor(name, list(shape), BF16, kind="Internal").ap()

    xm_d = din("xm", [NM, D]); xp_d = din("xp", [NP, D]); xs_d = din("xs", [TS, D])
    flag_d = din("flag", [128, 1])
    cT_d = din("cT", [128, KC, 2])
    wada_d = din("w_ada", [D, 6 * D]); badaT_d = din("b_adaT", [128, 48]); bada_d = din("b_ada", [1, 6 * D])
    gn1T_d = din("gn1T", [128, KC]); gn2T_d = din("gn2T", [128, KC])
    win_d = din("w_in", [D, NIN]); wout_d = din("w_out", [D, D]); wup_d = din("w_up", [D, DFF]); wdn_d = din("w_down", [DFF, D])
    bg_d = din("bg", [4, 2]); gq_d = din("gq", [128, 1]); gkt_d = din("gkt", [128, 128]); sinks_d = din("sinks_b", [128, 8])
    gmo_d = din("gmo", [128, 4])
    ck_d = din("ck", [128, 128]); cv_d = din("cv", [128, 128])
    stC_d = din("stC", [4, 128, 128]); stnT_d = din("stnT", [128, 4]); stm_d = din("stm", [4, 1])
    ident_d = din("ident", [128, 128]); maskT_d = din("maskT", [128, 128])
    tblA_d = din("tblA", [128, 2, 512]); tblB_d = din("tblB", [128, 2, 512]); diag4_d = din("diag4", [4, 4])

    ym_d = dout("ym", [NM, D]); ys_d = dout("ys", [TS, D])
    okp_d = dout("okp", [128, 128]); ovp_d = dout("ovp", [128, 128]); oCp_d = dout("oCp", [4, 128, 128])
    onp_d = dout("onp", [128, 4]); omp_d = dout("omp", [4, 1])
    oks_d = dout("oks", [128, 128]); ovs_d = dout("ovs", [128, 128]); oCs_d = dout("oCs", [4, 128, 128])
    ons_d = dout("ons", [128, 4]); oms_d = dout("oms", [4, 1])

    wup_s = dscr("wup_s", [D, DFF]); wdn_s = dscr("wdn_s", [DFF, D])

    with es:
        S = Sched(nc)
        S.limit = limit
        cnt = [0]

        def sb(shape, dt, name=None):
            cnt[0] += 1
            name = "sb_" + (name or ("t%d" % cnt[0]))
            return Tl(es.enter_context(nc.sbuf_tensor(name, list(shape), dt)), name)

        banks = [Tl(es.enter_context(nc.psum_tensor("psb%d" % i, [128, 512], F32)), "psb%d" % i, True) for i in range(8)]
        bank_i = [0]
        bank_excl = set()

        bank_ia = [0]

        cur_pool = ["B"]
        bank_im = {"M": 0, "W": 0}

        def nps(pool=None):
            pool = pool or cur_pool[0]
            if pool == "A":
                i = bank_ia[0] % 4
                bank_ia[0] += 1
                return banks[i]
            if pool in ("M", "W"):
                i = (4 if pool == "M" else 6) + bank_im[pool] % 2
                bank_im[pool] += 1
                return banks[i]
            while True:
                i = 4 + bank_i[0] % 4
                bank_i[0] += 1
                if i not in bank_excl:
                    return banks[i]

        dmac = [0]

        def dkey(p):
            dmac[0] += 1
            return "%s%d" % (p, dmac[0])

        dram_out_bufs = []

        def load(eng, dst_tl, dst_ap, src_ap, key=None):
            S.op(eng, lambda e: e.dma_start(out=dst_ap, in_=src_ap), w=[dst_tl.b], dma=key or dkey("l"))

        def store(eng, src_tl, src_ap, dst_ap, key=None):
            ob = Buf("o")
            dram_out_bufs.append(ob)
            S.op(eng, lambda e: e.dma_start(out=dst_ap, in_=src_ap), r=[src_tl.b], w=[ob], dma=key or dkey("s"))

        ident_f = sb([128, 128], F32, "ident_f"); ident_b = sb([128, 128], BF16, "ident_b")
        maskT = sb([128, 128], F32, "maskT")
        tblA = sb([128, 2, 512], BF16, "tblA"); tblB = sb([128, 2, 512], BF16, "tblB")
        diag4 = sb([4, 4], F32, "diag4"); ones4 = sb([4, 128], F32, "ones4"); ones41 = sb([4, 1], F32, "ones41"); ones1 = sb([1, 128], F32, "ones1")
        flag = sb([128, 1], F32, "flag")
        cT = sb([128, KC, 2], F32, "cT"); siluT = sb([128, KC, 2], BF16, "siluT")
        badaT = sb([128, 48], F32, "badaT"); modT = sb([128, 48, 2], F32, "modT")
        gn1T = sb([128, KC], F32, "gn1T"); gn2T = sb([128, KC], F32, "gn2T")
        gam = sb([128, 2, 2, KC], F32, "gam")
        bg = sb([4, 2], F32, "bg"); nbf = sb([4, 1], F32, "nbf")
        gq8 = sb([128, 1], F32, "gq8"); gkt = sb([128, 128], F32, "gkt")
        esink = sb([128, 8], F32, "esink"); gmo = sb([128, 4], F32, "gmo")
        win = sb([128, KC, NIN], BF16, "win"); wout = sb([128, KC, D], BF16, "wout")

        load("sp", ident_f, ident_f[:], ident_d)
        load("sp", maskT, maskT[:], maskT_d)
        load("pool", tblA, tblA[:], tblA_d)
        load("pool", tblB, tblB[:], tblB_d)
        load("sp", diag4, diag4[:], diag4_d)
        load("sp", flag, flag[:], flag_d)
        load("sp", cT, cT[:], cT_d)
        load("sp", badaT, badaT[:], badaT_d)
        load("sp", gn1T, gn1T[:], gn1T_d)
        load("sp", gn2T, gn2T[:], gn2T_d)
        load("sp", bg, bg[:], bg_d)
        load("sp", gq8, gq8[:], gq_d)
        load("sp", gkt, gkt[:], gkt_d)
        load("sp", esink, esink[:], sinks_d)
        load("sp", gmo, gmo[:], gmo_d)
        S.op("dve", lambda e: e.tensor_copy(out=ident_b[:], in_=ident_f[:]), r=[ident_f.b], w=[ident_b.b])
        S.op("dve", lambda e: e.memset(ones4[:], 1.0), w=[ones4.b])
        S.op("dve", lambda e: e.memset(ones41[:], 1.0), w=[ones41.b])
        S.op("dve", lambda e: e.memset(ones1[:], 1.0), w=[ones1.b])
        S.op("dve", lambda e: e.tensor_scalar(out=gq8[:], in0=gq8[:], scalar1=0.125, scalar2=None, op0=ALU.mult), r=[gq8.b], w=[gq8.b])
        S.op("act", lambda e: e.activation(out=esink[:], in_=esink[:], func=AF.Exp), r=[esink.b], w=[esink.b])
        S.op("dve", lambda e: e.tensor_scalar(out=nbf[:], in0=bg[:, 1:2], scalar1=-1.0, scalar2=None, op0=ALU.mult), r=[bg.b], w=[nbf.b])

        NXS = 5
        xslots = [sb([128, D], F32, "x%d" % i) for i in range(NXS)]
        xs_sample = xslots[0]
        ga_bc = [[None, xslots[3]], [sb([128, D], F32, "ga_bc1"), sb([128, D], F32, "ga_bc1s")]]
        xn = sb([128, D], BF16, "xn")
        crep = xn.t[:, :].rearrange("p (c m) -> p c m", m=128)
        st = sb([128, 16], F32, "st")
        hT = sb([128, KC, 512], BF16, "hT")
        mixT = sb([128, KC, 128], BF16, "mixT")
        h2T = sb([128, KC, 512], BF16, "h2T")
        ytmp = [sb([128, 512], F32, "ytmp%d" % i) for i in range(2)]
        qT = sb([128, 4, 512], BF16, "qT"); kT = sb([128, 4, 512], BF16, "kT")
        uT = sb([128, 32, 512], BF16, "uT")
        xn4 = [qT.t[:, 0:2, :].rearrange("p a b -> p (a b)"), qT.t[:, 2:4, :].rearrange("p a b -> p (a b)"),
               kT.t[:, 0:2, :].rearrange("p a b -> p (a b)"), kT.t[:, 2:4, :].rearrange("p a b -> p (a b)")]
        xn4b = [Buf("xn4_%d" % i) for i in range(4)]
        Kp4 = [uT.t[:, i, :] for i in range(4)]
        vext4 = [uT.t[:, 4 + 2 * i:6 + 2 * i, :].rearrange("p a b -> p (a b)")[:, 0:516].rearrange("p (h v) -> p h v", v=129) for i in range(4)]
        Kp4b = [Buf("Kp4_%d" % i) for i in range(4)]
        vext4b = [Buf("vext4_%d" % i) for i in range(4)]
        uTb = [Buf("uT%d" % i) for i in range(32)]
        ga1p = Tl(None, "ga1p")
        ga1p.t = uT.t[:, 0:4, :].rearrange("p a b -> p (a b)").bitcast(F32)
        ga1p.b = Buf("ga1p")
        ga_bc[0][0] = ga1p
        fslots = [sb([128, KC, 512], BF16, "fs%d" % i) for i in range(2)]
        g1 = sb([4, 512], F32, "g1"); g2 = sb([4, 512], F32, "g2"); g3 = sb([4, 512], F32, "g3"); Fg = sb([4, 512], F32, "Fg")
        Fc = sb([4, 1], F32, "Fc"); Mcat = sb([4, 8], F32, "Mcat"); Mc = sb([4, 1], F32, "Mc")
        bmx = sb([4, 4], F32, "bmx"); rho = sb([4, 4], F32, "rho"); rhod = sb([4, 4, 4], F32, "rhod")
        rho_bc = sb([128, 4, 4], F32, "rho_bc")
        eb = [sb([128, 8], F32, "eb%d" % i) for i in range(4)]
        Kp = sb([128, 512], BF16, "Kp")
        vext = sb([128, 4, 129], BF16, "vext")
        Eo = sb([128, 512], F32, "Eo")
        qn = sb([128, 512], BF16, "qn")
        knf = sb([128, 128], F32, "knf"); vf = sb([128, 128], F32, "vf")
        knb2 = sb([128, 2, 2, 64], BF16, "knb2")
        vsw = [sb([128, 2, 65], BF16, "vsw%d" % i) for i in range(2)]
        kTs = [sb([128, 2, 128], BF16, "kTs%d" % i) for i in range(2)]
        qTs = sb([128, 4, 128], BF16, "qTs")
        STt = sb([128, 4, 128], BF16, "STt")
        Cbf = sb([128, 4, 129], BF16, "Cbf"); Chat = sb([128, 4, 129], F32, "Chat")
        Pm = [sb([128, 512], BF16, "Pm%d" % i) for i in range(4)]
        mix = sb([128, D], BF16, "mix")
        rtmp = sb([128, 512], F32, "rtmp")
        sqt = rtmp
        junk_ap = rtmp.t[:, :].bitcast(BF16)
        nvec = sb([128, 4], F32, "nvec")
        junk2 = sb([128, D], BF16, "junk2")
        st3 = sb([128, 16], F32, "st3")
        st4 = st3
        mixWb = Buf("mixW")
        mo = sb([4, 1], F32, "mo")

        S.op("dve", lambda e: e.memset(vext[:], 1.0), w=[vext.b])
        for i in range(2):
            S.op("dve", lambda e, i=i: e.memset(vsw[i][:], 1.0), w=[vsw[i].b])

        sil_t = sb([128, KC, 2], F32, "sil_t")
        S.op("act", lambda e: e.activation(out=sil_t[:], in_=cT[:], func=AF.Exp, scale=-1.0), r=[cT.b], w=[sil_t.b])
        S.op("dve", lambda e: e.tensor_scalar(out=sil_t[:], in0=sil_t[:], scalar1=1.0, scalar2=None, op0=ALU.add), r=[sil_t.b], w=[sil_t.b])
        S.op("dve", lambda e: e.reciprocal(out=sil_t[:], in_=sil_t[:]), r=[sil_t.b], w=[sil_t.b])
        S.op("dve", lambda e: e.tensor_tensor(out=siluT[:], in0=sil_t[:], in1=cT[:], op=ALU.mult), r=[sil_t.b, cT.b], w=[siluT.b])

        stg = []
        for i in range(2):
            tl = Tl(None, "stg%d" % i)
            tl.t = uT.t[:, 8 + 12 * i:20 + 12 * i, :].rearrange("p a b -> p (a b)").bitcast(F32)
            tl.b = Buf("stg%d" % i)
            stg.append(tl)
        for kc in range(KC):
            sg = stg[kc % 2]
            load("sp", sg, sg[:, 0:NIN], win_d[kc * 128:(kc + 1) * 128, :], key="stg%d" % (kc % 2))
            S.op("dve", lambda e, kc=kc, sg=sg: e.tensor_copy(out=win[:, kc, :], in_=sg[:, 0:NIN]), r=[sg.b], w=[win.b], cost=1.7)
        for kc2 in range(KC // 2):
            sg = stg[kc2 % 2]
            load("sp", sg, sg[:, 0:2 * D].rearrange("p (c n) -> p c n", n=D), wout_d[kc2 * 256:(kc2 + 1) * 256, :].rearrange("(c p) n -> p c n", p=128), key="stg%d" % (kc2 % 2))
            S.op("dve", lambda e, kc2=kc2, sg=sg: e.tensor_copy(out=wout[:, 2 * kc2:2 * kc2 + 2, :], in_=sg[:, 0:2 * D].rearrange("p (c n) -> p c n", n=D)),
                 r=[sg.b], w=[wout.b], cost=1.3)
        mod_ps = nps()
        bank_excl.add(4)
        for j in range(12):
            sl = fslots[j % 2]
            load("pool", sl, sl[:], wada_d[:, j * 512:(j + 1) * 512].rearrange("(c p) n -> p c n", p=128), key="wa%d" % (j % 2))
            seg = j // 2
            if seg in (2, 5):
                gi = 0 if seg == 2 else 1
                col = (j % 2) * 512
                load("sp", rtmp, rtmp[0:1, :], bada_d[:, j * 512:(j + 1) * 512])
                for r in range(2):
                    S.op("dve", lambda e, r=r: e.tensor_copy(out=crep, in_=siluT[:, :, r:r + 1].to_broadcast([128, KC, 128])),
                         r=[siluT.b], w=[xn.b])
                    pb = nps()

                    def mmg(e, sl=sl, pb=pb):
                        i = None
                        for kc in range(KC):
                            i = e.matmul(pb[:, :], lhsT=crep[:, kc, :], rhs=sl[:, kc, :], start=(kc == 0), stop=False)
                        i = e.matmul(pb[:, :], lhsT=ones1[0:1, :], rhs=rtmp[0:1, :], start=False, stop=True)
                        return i
                    S.op("pe", mmg, r=[xn.b, sl.b, rtmp.b, ones1.b], w=[pb.b])
                    if gi == 0 and r == 1:
                        tgt, tcol = ytmp[j % 2], 0
                    else:
                        tgt, tcol = ga_bc[gi][r], col
                    S.op("act", lambda e, tgt=tgt, pb=pb, tcol=tcol: e.activation(out=tgt[:, tcol:tcol + 512], in_=pb[:, :], func=AF.Copy),
                         r=[pb.b], w=[tgt.b])
            else:
                def mmf(e, sl=sl, j=j):
                    i = None
                    for q in range(4):
                        nchunk = j * 4 + q
                        for kc in range(KC):
                            i = e.matmul(mod_ps[:, nchunk * 2:nchunk * 2 + 2], lhsT=sl[:, kc, q * 128:(q + 1) * 128], rhs=siluT[:, kc, :],
                                         start=(kc == 0), stop=(kc == KC - 1))
                    return i
                S.op("pe", mmf, r=[sl.b, siluT.b], w=[mod_ps.b])
        for j0 in (0, 24):
            S.op("dve", lambda e, j0=j0: e.tensor_tensor(out=modT[:, j0:j0 + 16, :], in0=mod_ps[:, 2 * j0:2 * j0 + 32].rearrange("p (j r) -> p j r", r=2),
                                                         in1=badaT[:, j0:j0 + 16].unsqueeze(2).to_broadcast([128, 16, 2]), op=ALU.add),
                 r=[mod_ps.b, badaT.b], w=[modT.b])
        for n, (gT, sc0) in enumerate(((gn1T, 8), (gn2T, 32))):
            for r in range(2):
                S.op("dve", lambda e, n=n, r=r, gT=gT, sc0=sc0: e.scalar_tensor_tensor(
                    out=gam[:, n, r, :], in0=modT[:, sc0:sc0 + 8, r], scalar=1.0, in1=gT[:], op0=ALU.add, op1=ALU.mult),
                    r=[modT.b, gT.b], w=[gam.b])
        bank_excl.discard(4)

        def sh_ap(n, r, c):
            base = 0 if n == 0 else 24
            return modT[:, base + c, r:r + 1]

        for h in range(4):
            S.op("dve", lambda e, h=h: e.tensor_scalar(out=wout[:, h, :], in0=wout[:, h, :], scalar1=gmo[:, h:h + 1], scalar2=None, op0=ALU.mult),
                 r=[wout.b, gmo.b], w=[wout.b])
        scrb = Buf("scr")
        def issue_scratch_casts():
            for q in range(4):
                S.op("pool", lambda e, q=q: e.dma_start(out=wup_s[q * 256:(q + 1) * 256, :], in_=wup_d[q * 256:(q + 1) * 256, :]), w=[scrb], dma="scr")
            for q in range(4):
                S.op("pool", lambda e, q=q: e.dma_start(out=wdn_s[q * 1024:(q + 1) * 1024, :], in_=wdn_d[q * 1024:(q + 1) * 1024, :]), w=[scrb], dma="scr")


        def rstd_from_ss(ss_ap, out_ap, n, tl):
            S.op("act", lambda e: e.activation(out=out_ap, in_=ss_ap, func=AF.Ln, scale=1.0 / n, bias=EPS), r=[tl.b], w=[tl.b])
            S.op("act", lambda e: e.activation(out=out_ap, in_=out_ap, func=AF.Exp, scale=-0.5), r=[tl.b], w=[tl.b])

        def norm_T(xt, T, dstT, col0, n, r):
            S.op("act", lambda e: e.activation(out=junk2[0:T, :], in_=xt[0:T, :], func=AF.Square, accum_out=st[0:T, 0:1]),
                 r=[xt.b], w=[st.b], cost=1.1)
            rstd_from_ss(st[0:T, 0:1], st[0:T, 1:2], D, st)
            S.op("dve", lambda e: e.tensor_scalar(out=xn[0:T, :], in0=xt[0:T, :], scalar1=st[0:T, 1:2], scalar2=None, op0=ALU.mult),
                 r=[xt.b, st.b], w=[xn.b], cost=0.7)
            pb = nps()
            pbb = pb[:, :].bitcast(BF16)

            def tr(e):
                i = None
                for c in range(KC):
                    i = e.transpose(out=pbb[:, c * 128:c * 128 + T], in_=xn[0:T, c * 128:(c + 1) * 128], identity=ident_b[0:T, 0:T])
                return i
            S.op("pe", tr, r=[xn.b, ident_b.b], w=[pb.b], cost=0.6)
            base = 0 if n == 0 else 24
            pv3 = pbb[:, :].rearrange("p (c t) -> p c t", t=128)[:, :, 0:T]
            S.op("dve", lambda e: e.tensor_tensor(out=dstT[:, :, col0:col0 + T], in0=pv3, in1=gam[:, n, r, :].unsqueeze(2).to_broadcast([128, KC, T]), op=ALU.mult),
                 r=[pb.b, gam.b], w=[dstT.b], cost=1.2)
            S.op("dve", lambda e: e.tensor_tensor(out=dstT[:, :, col0:col0 + T], in0=dstT[:, :, col0:col0 + T],
                                                  in1=modT[:, base:base + KC, r].unsqueeze(2).to_broadcast([128, KC, T]), op=ALU.add),
                 r=[dstT.b, modT.b], w=[dstT.b], cost=0.7)

        def norm_T4(xts, dstT, n, r):
            T = 128
            for b in range(4):
                S.op("act", lambda e, b=b: e.activation(out=junk2[:, :], in_=xts[b][:, :], func=AF.Square, accum_out=st4[:, b:b + 1]),
                     r=[xts[b].b], w=[st4.b], cost=1.1)
            rstd_from_ss(st4[:, 0:4], st4[:, 4:8], D, st4)
            for b in range(4):
                S.op("act", lambda e, b=b: e.activation(out=xn4[b], in_=xts[b][:, :], func=AF.Identity, scale=st4[:, 4 + b:5 + b]),
                     r=[xts[b].b, st4.b], w=[xn4b[b], (qT.b if b < 2 else kT.b)], cost=1.1)
            base = 0 if n == 0 else 24
            for pair in range(2):
                pbs_ = {}
                for b in (2 * pair, 2 * pair + 1):
                    pbs_[b] = nps()
                    pbb = pbs_[b][:, :].bitcast(BF16)

                    def tr(e, b=b, pbb=pbb):
                        i = None
                        for c in range(KC):
                            i = e.transpose(out=pbb[:, c * 128:(c + 1) * 128], in_=xn4[b][:, c * 128:(c + 1) * 128], identity=ident_b[:, :])
                        return i
                    S.op("pe", tr, r=[xn4b[b], ident_b.b], w=[pbs_[b].b], cost=0.6)
                for b in (2 * pair, 2 * pair + 1):
                    pv3 = pbs_[b][:, :].bitcast(BF16).rearrange("p (c t) -> p c t", t=128)
                    S.op("dve", lambda e, b=b, pv3=pv3: e.tensor_tensor(out=dstT[:, :, b * 128:(b + 1) * 128], in0=pv3,
                                                                       in1=gam[:, n, r, :].unsqueeze(2).to_broadcast([128, KC, 128]), op=ALU.mult),
                         r=[pbs_[b].b, gam.b], w=[dstT.b], cost=1.2)
                    S.op("dve", lambda e, b=b: e.tensor_tensor(out=dstT[:, :, b * 128:(b + 1) * 128], in0=dstT[:, :, b * 128:(b + 1) * 128],
                                                               in1=modT[:, base:base + KC, r].unsqueeze(2).to_broadcast([128, KC, 128]), op=ALU.add),
                         r=[dstT.b, modT.b], w=[dstT.b], cost=0.7)

        def gates(NT, T, NB, hsrc=None, defer_tail=False):
            hsrc = hT if hsrc is None else hsrc
            Fin_ap = Fc[:, 0:1]
            Min_ap = Mc[:, 0:1]
            pi = nps(); pf = nps()

            def mg(e, p, c0):
                i = None
                for kc in range(KC):
                    i = e.matmul(p[0:4, 0:NT], lhsT=win[:, kc, c0:c0 + 4], rhs=hsrc[:, kc, 0:NT], start=(kc == 0), stop=(kc == KC - 1))
                return i
            S.op("pe", lambda e: mg(e, pi, C_MI), r=[win.b, hsrc.b], w=[pi.b])
            S.op("pe", lambda e: mg(e, pf, C_MF), r=[win.b, hsrc.b], w=[pf.b])
            S.op("act", lambda e: e.activation(out=g1[:, 0:NT], in_=pi[0:4, 0:NT], func=AF.Identity, bias=bg[:, 0:1]), r=[pi.b, bg.b], w=[g1.b])
            S.op("act", lambda e: e.activation(out=g2[:, 0:NT], in_=pf[0:4, 0:NT], func=AF.Exp, scale=-1.0, bias=nbf[:, 0:1]), r=[pf.b, nbf.b], w=[g2.b])
            S.op("act", lambda e: e.activation(out=g2[:, 0:NT], in_=g2[:, 0:NT], func=AF.Ln, bias=1.0), r=[g2.b], w=[g2.b])
            S.op("dve", lambda e: e.tensor_tensor_scan(out=Fg[:, 0:NT], data0=ones4[:, 0:1].to_broadcast([4, NT]), data1=g2[:, 0:NT],
                                                       initial=Fin_ap, op0=ALU.mult, op1=ALU.subtract),
                 r=[ones4.b, g2.b, Fc.b], w=[Fg.b])
            S.op("dve", lambda e: e.tensor_tensor(out=g1[:, 0:NT], in0=g1[:, 0:NT], in1=Fg[:, 0:NT], op=ALU.subtract), r=[g1.b, Fg.b], w=[g1.b])
            S.op("dve", lambda e: e.tensor_reduce(out=bmx[:, 0:NB], in_=g1[:, 0:NT].rearrange("p (b t) -> p b t", t=T), axis=AX.X, op=ALU.max),
                 r=[g1.b], w=[bmx.b])
            S.op("dve", lambda e: e.tensor_copy(out=Mcat[:, 0:1], in_=Min_ap), r=[Mc.b], w=[Mcat.b])
            S.op("dve", lambda e: e.tensor_tensor_scan(out=Mcat[:, 1:1 + NB], data0=bmx[:, 0:NB], data1=bmx[:, 0:NB], initial=Min_ap,
                                                       op0=ALU.max, op1=ALU.max),
                 r=[bmx.b, Mc.b], w=[Mcat.b])
            S.op("dve", lambda e: e.tensor_tensor(out=rho[:, 0:NB], in0=Mcat[:, 0:NB], in1=Mcat[:, 1:1 + NB], op=ALU.subtract), r=[Mcat.b], w=[rho.b])
            S.op("act", lambda e: e.activation(out=rho[:, 0:NB], in_=rho[:, 0:NB], func=AF.Exp), r=[rho.b], w=[rho.b])
            mb = Mcat[:, 1:1 + NB].unsqueeze(2).to_broadcast([4, NB, T])
            S.op("dve", lambda e: e.tensor_tensor(out=g2[:, 0:NT].rearrange("p (b t) -> p b t", t=T), in0=g1[:, 0:NT].rearrange("p (b t) -> p b t", t=T),
                                                  in1=mb, op=ALU.subtract), r=[g1.b, Mcat.b], w=[g2.b])
            S.op("act", lambda e: e.activation(out=g2[:, 0:NT], in_=g2[:, 0:NT], func=AF.Exp, bias=float(np.log(128.0 ** -0.5))), r=[g2.b], w=[g2.b])
            S.op("dve", lambda e: e.tensor_tensor(out=g3[:, 0:NT].rearrange("p (b t) -> p b t", t=T), in0=Fg[:, 0:NT].rearrange("p (b t) -> p b t", t=T),
                                                  in1=mb, op=ALU.add), r=[Fg.b, Mcat.b], w=[g3.b])
            S.op("act", lambda e: e.activation(out=g3[:, 0:NT], in_=g3[:, 0:NT], func=AF.Exp, scale=-1.0), r=[g3.b], w=[g3.b])
            if defer_tail:
                return lambda: gates_tail(NT, T, NB)
            gates_tail(NT, T, NB)

        def gates_tail(NT, T, NB):
            for b in range(NB):
                pt = nps()

                def trg(e, b=b, pt=pt):
                    e.transpose(out=pt[0:T, 0:4], in_=g2[:, b * T:(b + 1) * T], identity=ident_f[0:4, 0:4])
                    return e.transpose(out=pt[0:T, 4:8], in_=g3[:, b * T:(b + 1) * T], identity=ident_f[0:4, 0:4])
                S.op("pe", trg, r=[g2.b, g3.b, ident_f.b], w=[pt.b])
                S.op("dve", lambda e, b=b, pt=pt: e.tensor_copy(out=eb[b][0:T, :], in_=pt[0:T, 0:8]), r=[pt.b], w=[eb[b].b])
            S.op("dve", lambda e: e.tensor_tensor(out=rhod[:, 0:NB, :], in0=rho[:, 0:NB].unsqueeze(2).to_broadcast([4, NB, 4]),
                                                  in1=diag4[:].unsqueeze(1).to_broadcast([4, NB, 4]), op=ALU.mult),
                 r=[rho.b, diag4.b], w=[rhod.b])
            pr = nps()
            S.op("pe", lambda e: e.matmul(pr[:, 0:NB * 4], lhsT=ones4[:, :], rhs=rhod[:, 0:NB, :].rearrange("p b h -> p (b h)"), start=True, stop=True),
                 r=[ones4.b, rhod.b], w=[pr.b])
            S.op("dve", lambda e: e.tensor_copy(out=rho_bc[:, 0:NB, :].rearrange("p b h -> p (b h)"), in_=pr[:, 0:NB * 4]), r=[pr.b], w=[rho_bc.b])
            S.op("dve", lambda e: e.tensor_copy(out=Fc[:], in_=Fg[:, NT - 1:NT]), r=[Fg.b], w=[Fc.b])
            S.op("dve", lambda e: e.tensor_copy(out=Mc[:], in_=Mcat[:, NB:NB + 1]), r=[Mcat.b], w=[Mc.b])

        def tokmm(T, col_tok0, c0, ncols, pb, hsrc=None):
            hsrc = hT if hsrc is None else hsrc

            def f(e):
                i = None
                for kc in range(KC):
                    i = e.matmul(pb[0:T, 0:ncols], lhsT=hsrc[:, kc, col_tok0:col_tok0 + T], rhs=win[:, kc, c0:c0 + ncols],
                                 start=(kc == 0), stop=(kc == KC - 1))
                return i
            S.op("pe", f, r=[hsrc.b, win.b], w=[pb.b], cost=8 * max(ncols, 64) / 2200.0)

        def kv_state(T, b, col_tok0):
            pk = nps()
            tokmm(T, col_tok0, C_MK, 512, pk)
            for h in range(4):
                if h < 2:
                    S.op("act", lambda e, h=h: e.activation(out=Kp[0:T, h * 128:(h + 1) * 128], in_=pk[0:T, h * 128:(h + 1) * 128],
                                                           func=AF.Identity, scale=eb[b][0:T, h:h + 1]),
                         r=[pk.b, eb[b].b], w=[Kp.b])
                else:
                    S.op("dve", lambda e, h=h: e.tensor_scalar(out=Kp[0:T, h * 128:(h + 1) * 128], in0=pk[0:T, h * 128:(h + 1) * 128],
                                                                scalar1=eb[b][0:T, h:h + 1], scalar2=None, op0=ALU.mult),
                         r=[pk.b, eb[b].b], w=[Kp.b])
            pv = nps()
            tokmm(T, col_tok0, C_MV, 512, pv)
            S.op("act", lambda e: e.activation(out=vext[0:T, :, 0:128], in_=pv[0:T, :].rearrange("p (h v) -> p h v", v=128), func=AF.Copy),
                 r=[pv.b], w=[vext.b])

        def state_update(T, b, Kp_=None, Kpb=None, vext_=None, vextb=None):
            Kp_ = Kp.t if Kp_ is None else Kp_
            Kpb = Kp.b if Kpb is None else Kpb
            vext_ = vext.t if vext_ is None else vext_
            vextb = vext.b if vextb is None else vextb
            for hp in range(2):
                pc = nps()

                def f(e, hp=hp, pc=pc):
                    i = None
                    for hh in range(2):
                        h = hp * 2 + hh
                        i = e.matmul(pc[:, hh * 129:(hh + 1) * 129], lhsT=Kp_[0:T, h * 128:(h + 1) * 128], rhs=vext_[0:T, h, :], start=True, stop=True)
                    return i
                S.op("pe", f, r=[Kpb, vextb], w=[pc.b])
                for hh in range(2):
                    h = hp * 2 + hh
                    S.op("dve", lambda e, h=h, hh=hh, pc=pc: e.scalar_tensor_tensor(
                        out=Chat[:, h, :], in0=Chat[:, h, :], scalar=rho_bc[:, b, h:h + 1], in1=pc[:, hh * 129:(hh + 1) * 129],
                        op0=ALU.mult, op1=ALU.add), r=[Chat.b, rho_bc.b, pc.b], w=[Chat.b])

        def swa_kv(T, col_tok0, cur, want_f32, hsrc=None):
            pkv = nps()
            tokmm(T, col_tok0, C_AK, 256, pkv, hsrc)
            S.op("act", lambda e: e.activation(out=sqt[0:T, 0:128], in_=pkv[0:T, 0:128], func=AF.Square), r=[pkv.b], w=[sqt.b])
            S.op("dve", lambda e: e.tensor_reduce(out=st[0:T, 2:4], in_=sqt[0:T, 0:128].rearrange("p (k d) -> p k d", d=64), axis=AX.X, op=ALU.add),
                 r=[sqt.b], w=[st.b])
            rstd_from_ss(st[0:T, 2:4], st[0:T, 4:6], 64, st)
            S.op("dve", lambda e: e.tensor_tensor(out=knf[0:T, :].rearrange("p (k d) -> p k d", d=64), in0=pkv[0:T, 0:128].rearrange("p (k d) -> p k d", d=64),
                                                  in1=st[0:T, 4:6].unsqueeze(2).to_broadcast([T, 2, 64]), op=ALU.mult),
                 r=[pkv.b, st.b], w=[knf.b])
            S.op("dve", lambda e: e.tensor_tensor(out=knf[0:T, :], in0=knf[0:T, :], in1=gkt[0:T, :], op=ALU.mult), r=[knf.b, gkt.b], w=[knf.b])
            S.op("dve", lambda e: e.tensor_copy(out=knb2[0:T, :, :, :], in_=knf[0:T, :].rearrange("p (k d) -> p k d", d=64).unsqueeze(2).to_broadcast([T, 2, 2, 64])),
                 r=[knf.b], w=[knb2.b])
            S.op("act", lambda e: e.activation(out=vsw[cur][0:T, :, 0:64], in_=pkv[0:T, 128:256].rearrange("p (k d) -> p k d", d=64), func=AF.Copy),
                 r=[pkv.b], w=[vsw[cur].b])
            if want_f32:
                S.op("dve", lambda e: e.tensor_copy(out=vf[0:T, :], in_=pkv[0:T, 128:256]), r=[pkv.b], w=[vf.b])
            kt_from_knb2(T, cur)

        def kt_from_knb2(T, cur):
            pt = nps()
            ptb = pt[:, :].bitcast(BF16)

            def f(e):
                i = None
                for kv in range(2):
                    i = e.transpose(out=ptb[:, kv * 128:kv * 128 + T], in_=knb2[0:T, kv, :, :].rearrange("p a d -> p (a d)"), identity=ident_b[0:T, 0:T])
                return i
            S.op("pe", f, r=[knb2.b, ident_b.b], w=[pt.b])
            S.op("act", lambda e: e.activation(out=kTs[cur][:, :, 0:T], in_=ptb[:, 0:256].rearrange("p (k t) -> p k t", t=128)[:, :, 0:T], func=AF.Copy),
                 r=[pt.b], w=[kTs[cur].b])

        def mixer_block(T, b, col_tok0, cur, TkA, r, want_f32, cbf_ready=False, cbf_next=False):
            prev = 1 - cur
            S.begin()
            cur_pool[0] = "M"
            ps_ = nps()

            def mms(e):
                i = None
                for h in range(4):
                    i = e.matmul(ps_[0:T, h * 128:h * 128 + T], lhsT=kT[:, h, col_tok0:col_tok0 + T], rhs=qT[:, h, col_tok0:col_tok0 + T], start=True, stop=True)
                return i
            S.op("pe", mms, r=[kT.b, qT.b], w=[ps_.b])
            for h in range(4):
                S.op("dve", lambda e, h=h: e.scalar_tensor_tensor(out=STt[0:T, h, 0:T], in0=ps_[0:T, h * 128:h * 128 + T], scalar=eb[b][0:T, h:h + 1],
                                                                   in1=maskT[0:T, 0:T], op0=ALU.mult, op1=ALU.mult),
                     r=[ps_.b, eb[b].b, maskT.b], w=[STt.b])
            kv_state(T, b, col_tok0)
            po = nps()
            tokmm(T, col_tok0, C_MO, 512, po)
            S.op("act", lambda e: e.activation(out=Eo[0:T, :], in_=po[0:T, :], func=AF.Exp, scale=-1.0), r=[po.b], w=[Eo.b])
            M1 = S.end()
            S.begin()
            cur_pool[0] = "W"
            pq = nps()
            tokmm(T, col_tok0, C_AQ, 512, pq)
            S.op("act", lambda e: e.activation(out=sqt[0:T, :], in_=pq[0:T, :], func=AF.Square), r=[pq.b], w=[sqt.b])
            S.op("dve", lambda e: e.tensor_reduce(out=st[0:T, 8:16], in_=sqt[0:T, :].rearrange("p (k d) -> p k d", d=64), axis=AX.X, op=ALU.add),
                 r=[sqt.b], w=[st.b])
            rstd_from_ss(st[0:T, 8:16], st[0:T, 8:16], 64, st)
            S.op("dve", lambda e: e.tensor_tensor(out=qn[0:T, :].rearrange("p (k d) -> p k d", d=64), in0=pq[0:T, :].rearrange("p (k d) -> p k d", d=64),
                                                  in1=st[0:T, 8:16].unsqueeze(2).to_broadcast([T, 8, 64]), op=ALU.mult),
                 r=[pq.b, st.b], w=[qn.b])
            pt = nps()
            ptb = pt[:, :].bitcast(BF16)

            def trq(e):
                i = None
                for p in range(4):
                    i = e.transpose(out=ptb[:, p * 128:p * 128 + T], in_=qn[0:T, p * 128:(p + 1) * 128], identity=ident_b[0:T, 0:T])
                return i
            S.op("pe", trq, r=[qn.b, ident_b.b], w=[pt.b])
            S.op("dve", lambda e: e.tensor_scalar(out=qTs[:, :, 0:T], in0=ptb[:, 0:512].rearrange("p (k t) -> p k t", t=128)[:, :, 0:T],
                                                  scalar1=gq8[:, 0:1], scalar2=None, op0=ALU.mult), r=[pt.b, gq8.b], w=[qTs.b])
            swa_kv(T, col_tok0, cur, want_f32)
            W1 = S.end()
            S.begin()
            cur_pool[0] = "M"
            def emit_cbf(bb):
                for h in range(4):
                    S.op("act", lambda e, h=h, bb=bb: e.activation(out=Cbf[:, h, :], in_=Chat[:, h, :], func=AF.Identity, scale=rho_bc[:, bb, h:h + 1]),
                         r=[Chat.b, rho_bc.b], w=[Cbf.b])
            if not cbf_ready:
                emit_cbf(b)
            pn = [nps(), nps()]
            for hp in range(2):
                def mmn(e, hp=hp):
                    i = None
                    for hh in range(2):
                        h = hp * 2 + hh
                        e.matmul(pn[hp][0:T, hh * 129:(hh + 1) * 129], lhsT=STt[0:T, h, 0:T], rhs=vext[0:T, h, :], start=True, stop=False)
                        i = e.matmul(pn[hp][0:T, hh * 129:(hh + 1) * 129], lhsT=qT[:, h, col_tok0:col_tok0 + T], rhs=Cbf[:, h, :], start=False, stop=True)
                    return i
                S.op("pe", mmn, r=[STt.b, vext.b, qT.b, Cbf.b], w=[pn[hp].b])
            for hp in range(2):
                S.op("dve", lambda e, hp=hp: e.tensor_copy(out=st2[0:T, hp * 2:hp * 2 + 2], in_=pn[hp][0:T, 0:258].rearrange("p (h v) -> p h v", v=129)[:, :, 128]),
                     r=[pn[hp].b], w=[st2.b])
            S.op("dve", lambda e: e.tensor_scalar(out=st2[0:T, 4:8], in0=st2[0:T, 0:4], scalar1=-1.0, scalar2=None, op0=ALU.mult), r=[st2.b], w=[st2.b])
            S.op("dve", lambda e: e.tensor_tensor(out=st2[0:T, 0:4], in0=st2[0:T, 0:4], in1=st2[0:T, 4:8], op=ALU.max), r=[st2.b], w=[st2.b])
            S.op("dve", lambda e: e.tensor_tensor(out=st2[0:T, 0:4], in0=st2[0:T, 0:4], in1=eb[b][0:T, 4:8], op=ALU.max), r=[st2.b, eb[b].b], w=[st2.b])
            S.op("dve", lambda e: e.reciprocal(out=st2[0:T, 0:4], in_=st2[0:T, 0:4]), r=[st2.b], w=[st2.b])
            for h in range(4):
                hp, hh = divmod(h, 2)
                S.op("act", lambda e, h=h, hp=hp, hh=hh: e.activation(out=junk2[0:T, h * 128:(h + 1) * 128], in_=pn[hp][0:T, hh * 129:hh * 129 + 128],
                                                                     func=AF.Square, accum_out=st2[0:T, 8 + h:9 + h]),
                     r=[pn[hp].b], w=[st2.b])
            S.op("dve", lambda e: e.tensor_tensor(out=st2[0:T, 8:12], in0=st2[0:T, 8:12], in1=st2[0:T, 0:4], op=ALU.mult), r=[st2.b], w=[st2.b])
            S.op("dve", lambda e: e.tensor_tensor(out=st2[0:T, 8:12], in0=st2[0:T, 8:12], in1=st2[0:T, 0:4], op=ALU.mult), r=[st2.b], w=[st2.b])
            S.op("act", lambda e: e.activation(out=st2[0:T, 8:12], in_=st2[0:T, 8:12], func=AF.Ln, scale=1.0 / 128, bias=EPS), r=[st2.b], w=[st2.b])
            S.op("act", lambda e: e.activation(out=st2[0:T, 8:12], in_=st2[0:T, 8:12], func=AF.Exp, scale=0.5), r=[st2.b], w=[st2.b])
            S.op("dve", lambda e: e.reciprocal(out=st2[0:T, 4:8], in_=st2[0:T, 0:4]), r=[st2.b], w=[st2.b])
            S.op("dve", lambda e: e.tensor_tensor(out=st2[0:T, 8:12], in0=st2[0:T, 8:12], in1=st2[0:T, 4:8], op=ALU.mult), r=[st2.b], w=[st2.b])
            S.op("dve", lambda e: e.tensor_scalar(out=Eo[0:T, :], in0=Eo[0:T, :], scalar1=1.0, scalar2=None, op0=ALU.add), r=[Eo.b], w=[Eo.b])
            S.op("dve", lambda e: e.tensor_tensor(out=Eo[0:T, :].rearrange("p (h v) -> p h v", v=128), in0=Eo[0:T, :].rearrange("p (h v) -> p h v", v=128),
                                                  in1=st2[0:T, 8:12].unsqueeze(2).to_broadcast([T, 4, 128]), op=ALU.mult), r=[Eo.b, st2.b], w=[Eo.b])
            S.op("dve", lambda e: e.reciprocal(out=Eo[0:T, :], in_=Eo[0:T, :]), r=[Eo.b], w=[Eo.b])
            for hp in range(2):
                S.op("dve", lambda e, hp=hp: e.tensor_tensor(out=mix[0:T, hp * 256:(hp + 1) * 256].rearrange("p (h v) -> p h v", v=128),
                                                             in0=pn[hp][0:T, 0:258].rearrange("p (h v) -> p h v", v=129)[:, :, 0:128],
                                                             in1=Eo[0:T, hp * 256:(hp + 1) * 256].rearrange("p (h v) -> p h v", v=128), op=ALU.mult),
                     r=[pn[hp].b, Eo.b], w=[mix.b])
            state_update(T, b)
            if cbf_next:
                emit_cbf(b + 1)
            M2 = S.end()
            S.begin()
            cur_pool[0] = "W"
            tiles = [(prev, TkA, tblA), (cur, T, tblB)]
            for kv in range(2):
                for ti, (kb, Tk, tbl) in enumerate(tiles):
                    pls = [nps(), nps()]

                    def mml(e, kv=kv, kb=kb, Tk=Tk, pls=pls):
                        i = None
                        for par in range(2):
                            for a in range(2):
                                g = 2 * a + par
                                hd = kv * 4 + g
                                p, odd = divmod(hd, 2)
                                assert odd == par
                                r0 = 64 * odd
                                i = e.matmul(pls[par][0:Tk, a * 128:a * 128 + T], lhsT=kTs[kb][r0:r0 + 64, kv, 0:Tk], rhs=qTs[r0:r0 + 64, p, 0:T], start=True, stop=True)
                        return i
                    S.op("pe", mml, r=[kTs[kb].b, qTs.b], w=[pls[0].b, pls[1].b])
                    P = Pm[kv * 2 + ti]
                    for par in range(2):
                        S.op("act", lambda e, P=P, pls=pls, Tk=Tk, par=par: e.activation(
                            out=P[0:Tk, :].rearrange("p (a b t) -> p a b t", b=2, t=128)[:, :, par, 0:T],
                            in_=pls[par][0:Tk, 0:256].rearrange("p (a t) -> p a t", t=128)[:, :, 0:T], func=AF.Exp),
                            r=[pls[par].b], w=[P.b])
                    S.op("dve", lambda e, P=P, tbl=tbl, kv=kv, Tk=Tk: e.tensor_tensor(
                        out=P[0:Tk, :].rearrange("p (g t) -> p g t", t=128)[:, :, 0:T], in0=P[0:Tk, :].rearrange("p (g t) -> p g t", t=128)[:, :, 0:T],
                        in1=tbl[0:Tk, kv, :].rearrange("p (g t) -> p g t", t=128)[:, :, 0:T], op=ALU.mult), r=[P.b, tbl.b], w=[P.b])
            for kv in range(2):
                pa = nps()

                def mmpv(e, kv=kv, pa=pa):
                    i = None
                    for g in range(4):
                        for ti, (kb, Tk, tbl) in enumerate(tiles):
                            P = Pm[kv * 2 + ti]
                            i = e.matmul(pa[0:T, g * 65:(g + 1) * 65], lhsT=P[0:Tk, g * 128:g * 128 + T], rhs=vsw[kb][0:Tk, kv, :], start=(ti == 0), stop=(ti == 1))
                    return i
                S.op("pe", mmpv, r=[Pm[kv * 2].b, Pm[kv * 2 + 1].b, vsw[0].b, vsw[1].b], w=[pa.b])
                pav = pa[0:T, 0:260].rearrange("p (g v) -> p g v", v=65)
                S.op("dve", lambda e, pav=pav, kv=kv: e.tensor_tensor(out=st3[0:T, 12:16], in0=pav[:, :, 64], in1=esink[0:T, kv * 4:(kv + 1) * 4], op=ALU.add),
                     r=[pa.b, esink.b], w=[st3.b])
                S.op("dve", lambda e: e.reciprocal(out=st3[0:T, 12:16], in_=st3[0:T, 12:16]), r=[st3.b], w=[st3.b])
                S.op("dve", lambda e, pav=pav, kv=kv: e.tensor_tensor(out=mix[0:T, 512 + kv * 256:512 + (kv + 1) * 256].rearrange("p (g d) -> p g d", d=64),
                                                                      in0=pav[:, :, 0:64], in1=st3[0:T, 12:16].unsqueeze(2).to_broadcast([T, 4, 64]), op=ALU.mult),
                     r=[pa.b, st3.b], w=[mixWb])
            W2 = S.end()
            cur_pool[0] = "B"
            S.interleave(M1 + M2, W1 + W2)
            pm_ = nps()
            pmb = pm_[:, :].bitcast(BF16)

            def trm(e):
                i = None
                for c in range(KC):
                    i = e.transpose(out=pmb[:, c * 128:c * 128 + T], in_=mix[0:T, c * 128:(c + 1) * 128], identity=ident_b[0:T, 0:T])
                return i
            S.op("pe", trm, r=[mix.b, mixWb, ident_b.b], w=[pm_.b])
            S.op("act", lambda e: e.activation(out=mixT[:, :, 0:T], in_=pmb[:, :].rearrange("p (c t) -> p c t", t=128)[:, :, 0:T], func=AF.Copy),
                 r=[pm_.b], w=[mixT.b])

        st2 = sb([128, 16], F32, "st2")

        def wout_res(T, col_tok0, xt, r, folded):
            for n in range(2):
                pb = nps()

                def f(e, n=n, pb=pb):
                    i = None
                    for kc in range(KC):
                        i = e.matmul(pb[0:T, :], lhsT=mixT[:, kc, 0:T], rhs=wout[:, kc, n * 512:(n + 1) * 512], start=(kc == 0), stop=(kc == KC - 1))
                    return i
                S.op("pe", f, r=[mixT.b, wout.b], w=[pb.b], cost=8 * 0.23)
                if folded:
                    S.op("dve", lambda e, n=n, pb=pb: e.tensor_tensor(out=xt[0:T, n * 512:(n + 1) * 512], in0=xt[0:T, n * 512:(n + 1) * 512], in1=pb[0:T, :], op=ALU.add),
                         r=[xt.b, pb.b], w=[xt.b], cost=0.7)
                else:
                    S.op("dve", lambda e, n=n, pb=pb: e.tensor_tensor(out=rtmp[0:T, :], in0=pb[0:T, :], in1=ytmp[n][0:T, :], op=ALU.mult),
                         r=[pb.b, ytmp[n].b], w=[rtmp.b])
                    S.op("dve", lambda e, n=n: e.tensor_tensor(out=xt[0:T, n * 512:(n + 1) * 512], in0=xt[0:T, n * 512:(n + 1) * 512], in1=rtmp[0:T, :], op=ALU.add),
                         r=[xt.b, rtmp.b], w=[xt.b])

        def fold_ga1():
            for kc in range(KC):
                S.op("dve", lambda e, kc=kc: e.tensor_tensor(out=wout[:, kc, :], in0=wout[:, kc, :], in1=ga_bc[0][0][:, :], op=ALU.mult),
                     r=[wout.b, ga_bc[0][0].b], w=[wout.b])

        yk = [0]

        def ffn_up(NT, h2src=None):
            h2src = h2T if h2src is None else h2src
            for g in range(8):
                sl = fslots[g % 2]
                S.op("sp", lambda e, sl=sl, g=g: e.dma_start(out=sl[:], in_=wup_s[:, g * 512:(g + 1) * 512].rearrange("(c p) n -> p c n", p=128)),
                     r=[scrb], w=[sl.b], dma="fs%d" % (g % 2), cost=3.0)
                for j in range(4):
                    pb = nps("A")

                    for half in range(2):
                        def f(e, sl=sl, j=j, pb=pb, half=half):
                            i = None
                            for kc in range(half * 4, half * 4 + 4):
                                i = e.matmul(pb[:, 0:NT], lhsT=sl[:, kc, j * 128:(j + 1) * 128], rhs=h2src[:, kc, 0:NT], start=(kc == 0), stop=(kc == KC - 1))
                            return i
                        S.op("pe", f, r=[sl.b, h2src.b], w=[pb.b], cost=4 * max(NT, 64) / 2200.0)
                    ub = uTb[g * 4 + j]
                    S.op("act", lambda e, g=g, j=j, pb=pb: e.activation(out=uT[:, g * 4 + j, 0:NT], in_=pb[:, 0:NT], func=AF.Relu), r=[pb.b], w=[ub] + ([ga1p.b] if g * 4 + j < 4 else []), cost=0.2 + NT / 1200.0)
                    S.op("dve", lambda e, g=g, j=j: e.tensor_tensor(out=uT[:, g * 4 + j, 0:NT], in0=uT[:, g * 4 + j, 0:NT], in1=uT[:, g * 4 + j, 0:NT], op=ALU.mult),
                         r=[ub], w=[ub], cost=0.1 + NT / 1900.0)

        def ffn_down(blocks, r):
            for n in range(2):
                pbs = [nps("A") for _ in blocks]
                for g in range(4):
                    sl = fslots[g % 2]
                    S.op("sp", lambda e, sl=sl, g=g, n=n: e.dma_start(out=sl[:], in_=wdn_s[g * 1024:(g + 1) * 1024, n * 512:(n + 1) * 512].rearrange("(c p) n -> p c n", p=128)),
                         r=[scrb], w=[sl.b], dma="fs%d" % (g % 2), cost=3.0)
                    for bi, (yd, T, c0, yb) in enumerate(blocks):
                        for half in range(2):
                            def f(e, sl=sl, g=g, bi=bi, T=T, c0=c0, pbs=pbs, half=half):
                                i = None
                                for j in range(half * 4, half * 4 + 4):
                                    i = e.matmul(pbs[bi][0:T, :], lhsT=uT[:, g * 8 + j, c0:c0 + T], rhs=sl[:, j, :], start=(g == 0 and j == 0), stop=(g == 3 and j == 7))
                                return i
                            S.op("pe", f, r=uTb[g * 8 + half * 4:g * 8 + half * 4 + 4] + [sl.b], w=[pbs[bi].b], cost=4 * 0.23)
                for bi, (yd, T, c0, yb) in enumerate(blocks):
                    yt = ytmp[yk[0] % 2]
                    yk[0] += 1
                    S.op("dve", lambda e, bi=bi, T=T, n=n, pbs=pbs, yt=yt: e.tensor_tensor(out=yt[0:T, :], in0=pbs[bi][0:T, :], in1=ga_bc[1][r][0:T, n * 512:(n + 1) * 512], op=ALU.mult),
                         r=[pbs[bi].b, ga_bc[1][r].b], w=[yt.b], cost=0.7)
                    S.op("pool", lambda e, yd=yd, T=T, n=n, yt=yt: e.dma_start(out=yd[:, n * 512:(n + 1) * 512], in_=yt[0:T, :], accum_op=ALU.add),
                         r=[yt.b], w=[yb], dma="ya_" + yt.b.name, cost=3.0)

        def qk_feature_major(NT):
            for which, c0, dst in ((0, C_MQ, qT), (1, C_MK, kT)):
                for h in range(4):
                    pb = nps()

                    def f(e, c0=c0, h=h, pb=pb):
                        i = None
                        for kc in range(KC):
                            i = e.matmul(pb[:, 0:NT], lhsT=win[:, kc, c0 + h * 128:c0 + (h + 1) * 128], rhs=hT[:, kc, 0:NT], start=(kc == 0), stop=(kc == KC - 1))
                        return i
                    S.op("pe", f, r=[win.b, hT.b], w=[pb.b], cost=8 * max(NT, 64) / 2200.0)
                    S.op("act", lambda e, dst=dst, h=h, pb=pb: e.activation(out=dst[:, h, 0:NT], in_=pb[:, 0:NT], func=AF.Copy), r=[pb.b],
                         w=[dst.b] + (xn4b[0:2] if which == 0 else xn4b[2:4]), cost=0.2 + NT / 1200.0)

        def write_state(oC, on, om, ok, ov, T_k):
            store("pool", Chat, Chat[:, :, 0:128], oC.rearrange("h d v -> d h v"))
            S.op("dve", lambda e: e.tensor_copy(out=nvec[:], in_=Chat[:, :, 128]), r=[Chat.b], w=[nvec.b])
            store("pool", nvec, nvec[:], on)
            S.op("dve", lambda e: e.tensor_tensor(out=mo[:], in0=Fc[:], in1=Mc[:], op=ALU.add), r=[Fc.b, Mc.b], w=[mo.b])
            store("pool", mo, mo[:], om)

        xi = [0]

        xorder = [1, 2, 4, 3, 0]

        def next_x(src_d, row0):
            s = xslots[xorder[xi[0] % NXS]]
            xi[0] += 1
            load("sp", s, s[:], src_d[row0:row0 + 128, :], key="x_" + s.b.name)
            return s

        pre_x0 = [None]
        T = TS
        if phase < 1:
            S.op("pool", lambda e: e.nop(), r=dram_out_bufs, force=True)
            S.emit()
            return nc
        if not skip_sample:
            load("sp", xs_sample, xs_sample[0:T, :], xs_d)
            load("sp", Chat, Chat[:, :, 0:128], stC_d.rearrange("h d v -> d h v"))
            load("sp", nvec, nvec[:], stnT_d)
            S.op("dve", lambda e: e.tensor_copy(out=Chat[:, :, 128], in_=nvec[:]), r=[nvec.b], w=[Chat.b])
            load("sp", Mc, Mc[:], stm_d)
            S.op("dve", lambda e: e.memset(Fc[:], 0.0), w=[Fc.b])
            load("sp", knf, knf[:], ck_d)
            load("sp", vf, vf[:], cv_d)
            S.op("dve", lambda e: e.tensor_copy(out=knb2[:, :, :, :], in_=knf[:, :].rearrange("p (k d) -> p k d", d=64).unsqueeze(2).to_broadcast([128, 2, 2, 64])),
                 r=[knf.b], w=[knb2.b])
            kt_from_knb2(128, 1)
            S.op("dve", lambda e: e.tensor_copy(out=vsw[1][:, :, 0:64], in_=vf[:, :].rearrange("p (k d) -> p k d", d=64)), r=[vf.b], w=[vsw[1].b])
            for (src, dst) in ((ck_d, oks_d), (cv_d, ovs_d)):
                ob = Buf("o"); dram_out_bufs.append(ob)
                S.op("pool", lambda e, src=src, dst=dst: e.dma_start(out=dst[0:128 - TS, :], in_=src[TS:128, :]), w=[ob], dma=dkey("s"))
            if NP >= 512:
                pre_x0[0] = [next_x(xp_d, b * 128) for b in range(4)]
            norm_T(xs_sample, T, hT, 0, 0, 1)
            qk_feature_major(T)
            gates(T, T, 1)
            mixer_block(T, 0, 0, 0, 128, 1, True)
            store("pool", knf, knf[0:T, :], oks_d[128 - TS:128, :])
            store("pool", vf, vf[0:T, :], ovs_d[128 - TS:128, :])
            write_state(oCs_d, ons_d, oms_d, None, None, None)
            wout_res(T, 0, xs_sample, 1, False)
            h2s = Tl(None, "h2s")
            h2s.t = xn.t[:, :].rearrange("p (c t) -> p c t", t=128)
            h2s.b = Buf("h2s")
            norm_T(xs_sample, T, h2s, 0, 1, 1)
            ysb = Buf("ys"); dram_out_bufs.append(ysb)
            S.op("pool", lambda e: e.dma_start(out=ys_d, in_=xs_sample[0:TS, :]), r=[xs_sample.b], w=[ysb], dma=dkey("s"))
            fold_ga1()
            S.begin()
            ffn_up(T, h2s)
            S.mark("up-1")
            ffn_down([(ys_d, T, 0, ysb)], 1)
            sample_ffn = S.end()
        else:
            fold_ga1()
            sample_ffn = []

        issue_scratch_casts()
        if phase < 2:
            S.replay(sample_ffn)
            S.op("pool", lambda e: e.nop(), r=dram_out_bufs, force=True)
            S.emit()
            return nc
        T = 128
        S.op("dve", lambda e: e.memset(Chat[:], 0.0), w=[Chat.b])
        S.op("dve", lambda e: e.memset(Fc[:], 0.0), w=[Fc.b])
        S.op("dve", lambda e: e.memset(Mc[:], 0.0), w=[Mc.b])
        n_pt = NP // 512
        if n_pt > 0:
            for b in range(4):
                S.op("dve", lambda e, b=b: e.memset(vext4[b], 1.0), w=[vext4b[b], Kp4b[b], ga1p.b])
        hbuf = [hT, h2T]

        def pre_N(t):
            S.begin()
            cur_pool[0] = "A"
            if t == 0 and pre_x0[0] is not None:
                xt = pre_x0[0]
            else:
                xt = [next_x(xp_d, t * 512 + b * 128) for b in range(4)]
            norm_T4(xt, hbuf[t % 2], 0, 0)
            cur_pool[0] = "B"
            return S.end()

        def pre_K(t):
            S.begin()
            cur_pool[0] = "B"
            hs = hbuf[t % 2]
            gtail = gates(512, T, 4, hs, defer_tail=True)
            for b in range(4):
                pv = nps()
                tokmm(T, b * 128, C_MV, 512, pv, hs)
                S.op("act", lambda e, b=b, pv=pv: e.activation(out=vext4[b][:, :, 0:128], in_=pv[:, :].rearrange("p (h v) -> p h v", v=128), func=AF.Copy),
                     r=[pv.b], w=[vext4b[b]], cost=0.65)
            gtail()
            for b in range(4):
                pk = nps()
                tokmm(T, b * 128, C_MK, 512, pk, hs)
                for h in range(4):
                    if h < 2:
                        S.op("act", lambda e, h=h, b=b, pk=pk: e.activation(out=Kp4[b][:, h * 128:(h + 1) * 128], in_=pk[:, h * 128:(h + 1) * 128],
                                                                           func=AF.Identity, scale=eb[b][:, h:h + 1]),
                             r=[pk.b, eb[b].b], w=[Kp4b[b]])
                    else:
                        S.op("dve", lambda e, h=h, b=b, pk=pk: e.tensor_scalar(out=Kp4[b][:, h * 128:(h + 1) * 128], in0=pk[:, h * 128:(h + 1) * 128],
                                                                             scalar1=eb[b][:, h:h + 1], scalar2=None, op0=ALU.mult),
                             r=[pk.b, eb[b].b], w=[Kp4b[b]])
            for b in range(4):
                state_update(T, b, Kp4[b], Kp4b[b], vext4[b], vext4b[b])
            if t == n_pt - 1:
                swa_kv(T, 3 * 128, 1, False, hs)
            cur_pool[0] = "B"
            return S.end()

        if n_pt > 0:
            S.replay(pre_N(0))
        for t in range(n_pt):
            Kl = pre_K(t)
            if t + 1 < n_pt:
                S.merge(pre_N(t + 1), Kl, a_first=20)
            else:
                S.replay(Kl)
        if n_pt > 0:
            S.op("dve", lambda e: e.memset(st3[:, 8:9], 0.0), r=Kp4b + vext4b, w=[st3.b, ga1p.b] + uTb[0:12])
        if n_pt > 0:
            S.op("dve", lambda e: e.tensor_scalar(out=Chat[:], in0=Chat[:], scalar1=flag[:, 0:1], scalar2=None, op0=ALU.mult), r=[Chat.b, flag.b], w=[Chat.b])
            S.op("dve", lambda e: e.tensor_scalar(out=Fc[:], in0=Fc[:], scalar1=flag[0:4, 0:1], scalar2=None, op0=ALU.mult), r=[Fc.b, flag.b], w=[Fc.b])
            S.op("dve", lambda e: e.tensor_scalar(out=Mc[:], in0=Mc[:], scalar1=flag[0:4, 0:1], scalar2=None, op0=ALU.mult), r=[Mc.b, flag.b], w=[Mc.b])
            S.op("dve", lambda e: e.tensor_scalar(out=vsw[1][:], in0=vsw[1][:], scalar1=flag[:, 0:1], scalar2=None, op0=ALU.mult), r=[vsw[1].b, flag.b], w=[vsw[1].b])
        else:
            S.op("dve", lambda e: e.memset(vsw[1][:], 0.0), w=[vsw[1].b])
            S.op("dve", lambda e: e.memset(kTs[1][:], 0.0), w=[kTs[1].b])

        if phase < 3:
            S.replay(sample_ffn)
            S.op("pool", lambda e: e.nop(), r=dram_out_bufs, force=True)
            S.emit()
            return nc
        n_mt = NM // 512
        curh = [0]

        def thread_B(t):
            S.begin()
            xt = xt_next[0]
            gtail = gates(512, T, 4, defer_tail=True)
            qk_feature_major(512)
            gtail()
            for b in range(4):
                last = (t == n_mt - 1 and b == 3)
                cur = curh[0]
                mixer_block(T, b, b * 128, cur, 128, 0, last, cbf_ready=(b > 0), cbf_next=(b < 3))
                if last:
                    store("pool", knf, knf[:, :], okp_d)
                    store("pool", vf, vf[:, :], ovp_d)
                wout_res(T, b * 128, xt[b], 0, True)
                curh[0] = 1 - cur
                if t == 0 and b == 0:
                    S.op("dve", lambda e, c=curh[0]: e.memset(vsw[c][:, :, 64:65], 1.0), w=[vsw[curh[0]].b])
            S.barrier("up%d" % (t - 1))
            blocks = []
            norm_T4(xt, h2T, 1, 0)
            for b in range(4):
                yd = ym_d[t * 512 + b * 128:t * 512 + (b + 1) * 128, :]
                yb = Buf("y"); dram_out_bufs.append(yb)
                S.op("sp", lambda e, yd=yd, xb=xt[b]: e.dma_start(out=yd, in_=xb[:, :]), r=[xt[b].b], w=[yb], dma="y_" + xt[b].b.name)
                blocks.append((yd, T, b * 128, yb))
            if t + 1 < n_mt:
                xt_next[0] = [next_x(xm_d, (t + 1) * 512 + b * 128) for b in range(4)]
                norm_T4(xt_next[0], hT, 0, 0)
            return S.end(), blocks

        def thread_A(t, blocks):
            S.begin()
            ffn_up(512)
            S.mark("up%d" % t)
            ffn_down(blocks, 0)
            return S.end()

        xt_next = [None]
        if n_mt > 0:
            xt_next[0] = [next_x(xm_d, b * 128) for b in range(4)]
            norm_T4(xt_next[0], hT, 0, 0)
            Bl, blocks = thread_B(0)
            S.merge(sample_ffn, Bl)
            for t in range(n_mt):
                Al = thread_A(t, blocks)
                if t + 1 < n_mt:
                    Bl, blocks = thread_B(t + 1)
                    S.merge(Al, Bl, a_first=24)
                else:
                    S.replay(Al)
        write_state(oCp_d, onp_d, omp_d, None, None, None)
        S.op("pool", lambda e: e.nop(), r=dram_out_bufs, force=True)
        S.op("sp", lambda e: e.nop(), r=dram_out_bufs, force=True)
        S.emit()
    return nc


_PROG = {}


def _tables():
    slopes = np.exp2(-8.0 * np.arange(1, 9, dtype=np.float64) / 8).astype(np.float64)
    s = np.arange(128)[:, None]
    t = np.arange(128)[None, :]
    distA = (t + 128 - s).astype(np.float64)
    allowA = (((s - 128) // 64) >= (t // 64) - 2)
    distB = np.abs(t - s).astype(np.float64)
    allowB = ((s // 64) <= (t // 64))
    tA = np.zeros((128, 2, 4, 128), np.float32)
    tB = np.zeros((128, 2, 4, 128), np.float32)
    for kv in range(2):
        for g in range(4):
            sl = slopes[kv * 4 + g]
            tA[:, kv, g, :] = np.exp(-sl * distA) * allowA
            tB[:, kv, g, :] = np.exp(-sl * distB) * allowB
    maskT = (s <= t).astype(np.float32)
    return tA.reshape(128, 2, 512), tB.reshape(128, 2, 512), maskT


def _prep_inputs(inp, NM, NP, n_seq_halves=2):
    f = lambda a: np.ascontiguousarray(a, dtype=np.float32)
    xp = inp["x_prompt"]
    Bn, L, _ = xp.shape
    tA, tB, maskT = _tables()
    common = {
        "w_ada": f(inp["w_ada"][0]), "b_adaT": f(inp["b_ada"][0].reshape(48, 128).T), "b_ada": f(inp["b_ada"][0].reshape(1, -1)),
        "gn1T": f(inp["g_norm1"][0].reshape(KC, 128).T), "gn2T": f(inp["g_norm2"][0].reshape(KC, 128).T),
        "w_in": f(inp["w_in"][0]), "w_out": f(inp["w_out"][0]), "w_up": f(inp["w_up"][0]), "w_down": f(inp["w_down"][0]),
        "bg": f(inp["b_gates"][0].reshape(2, 4).T),
        "gq": f(np.tile(inp["g_q"][0], 2).reshape(128, 1)),
        "gkt": f(np.tile(inp["g_k"][0][None, :], (128, 2))),
        "sinks_b": f(np.tile(inp["sinks"][0][None, :], (128, 1))),
        "gmo": f(inp["g_mlstm_out"][0].T),
        "ident": np.eye(128, dtype=np.float32), "maskT": maskT, "tblA": tA, "tblB": tB, "diag4": np.eye(4, dtype=np.float32),
    }
    maps = []
    for core in range(N_CORES):
        bi, half = divmod(core, 2)
        m = dict(common)
        m["xm"] = f(xp[bi, half * NM:(half + 1) * NM])
        if half == 0:
            m["xp"] = np.zeros((NP, D), np.float32)
        else:
            m["xp"] = f(xp[bi, 0:NP])
        m["flag"] = np.full((128, 1), float(half), np.float32)
        m["xs"] = f(inp["x_sample"][core])
        c2 = np.stack([inp["c_prompt"][bi], inp["c_sample"][core]], axis=0)
        m["cT"] = f(c2.reshape(2, KC, 128).transpose(2, 1, 0))
        m["ck"] = f(inp["cache_swa_k"][0, core].reshape(128, 128))
        m["cv"] = f(inp["cache_swa_v"][0, core].reshape(128, 128))
        m["stC"] = f(inp["state_mlstm_C"][0, core])
        m["stnT"] = f(inp["state_mlstm_n"][0, core].T)
        m["stm"] = f(inp["state_mlstm_m"][0, core].reshape(4, 1))
        maps.append(m)
    return maps


def kernel(**inputs):
    inp = {k: np.asarray(v) for k, v in inputs.items()}
    Bn, L, _ = inp["x_prompt"].shape
    NM = L // 2
    NP = NM
    key = (NM, NP)
    if key not in _PROG:
        _PROG[key] = build_program(NM, NP)
    nc = _PROG[key]
    maps = _prep_inputs(inp, NM, NP)
    res = run_bass_kernel_spmd(nc, maps, core_ids=list(range(N_CORES)))
    R = res.results
    y_p = np.zeros((Bn, L, D), np.float32)
    for core in range(N_CORES):
        bi, half = divmod(core, 2)
        y_p[bi, half * NM:(half + 1) * NM] = R[core]["ym"]
    y_s = np.stack([R[c]["ys"] for c in range(N_CORES)], axis=0)
    last = [2 * b + 1 for b in range(Bn)]
    swa_k_p = np.stack([R[c]["okp"].reshape(128, 2, 64) for c in last], 0)[None]
    swa_v_p = np.stack([R[c]["ovp"].reshape(128, 2, 64) for c in last], 0)[None]
    C_p = np.stack([R[c]["oCp"] for c in last], 0)[None]
    n_p = np.stack([R[c]["onp"].T for c in last], 0)[None]
    m_p = np.stack([R[c]["omp"].reshape(4) for c in last], 0)[None]
    allc = list(range(N_CORES))
    swa_k_s = np.stack([R[c]["oks"].reshape(128, 2, 64) for c in allc], 0)[None]
    swa_v_s = np.stack([R[c]["ovs"].reshape(128, 2, 64) for c in allc], 0)[None]
    C_s = np.stack([R[c]["oCs"] for c in allc], 0)[None]
    n_s = np.stack([R[c]["ons"].T for c in allc], 0)[None]
    m_s = np.stack([R[c]["oms"].reshape(4) for c in allc], 0)[None]
    outs = (y_p, y_s, swa_k_p, swa_v_p, C_p, n_p, m_p, swa_k_s, swa_v_s, C_s, n_s, m_s)
    return tuple(np.ascontiguousarray(o, dtype=np.float32) for o in outs)
```
